# Optimizing a Trainium2 kernel written in Bass

```python
import jax, jax.numpy as jnp
from jax import lax
import numpy as np

D_MODEL = 1024
BATCH = 8
SEQ = 4096
DEPTH = 2

D_MIX = D_MODEL
MLA_HEADS = 6
MLA_NOPE = 64
MLA_ROPE = 32
MLA_V = 64
Q_RANK = 256
KV_RANK = 128
MLA_WIDTH = MLA_HEADS * MLA_V
FOX_HEADS = 6
FOX_DIM = 64
FOX_WIDTH = FOX_HEADS * FOX_DIM
CONV_WIDTH = D_MIX - MLA_WIDTH - FOX_WIDTH
CONV_K = 3
D_FF = 2816
PLE_DIM = 256
BLOCK = 128
ROPE_THETA = 10000.0
EPS = 1e-6

IN_SIZES = (Q_RANK, KV_RANK, MLA_ROPE,
            CONV_WIDTH, CONV_WIDTH, CONV_WIDTH,
            FOX_WIDTH, FOX_WIDTH, FOX_WIDTH, FOX_HEADS)
N_IN = sum(IN_SIZES)
IN_SPLITS = [sum(IN_SIZES[:j + 1]) for j in range(len(IN_SIZES) - 1)]

kernel_name = "hymba_style_mla_shortconv_fox_hybrid"


def rmsnorm(x, g):
    xf = x.astype(jnp.float32)
    y = xf * lax.rsqrt(jnp.mean(xf * xf, axis=-1, keepdims=True) + EPS)
    return (y * g.astype(jnp.float32)).astype(x.dtype)


def rope_tables(positions):
    inv_freq = ROPE_THETA ** (-jnp.arange(0, MLA_ROPE, 2, dtype=jnp.float32) / MLA_ROPE)
    ang = positions.astype(jnp.float32)[..., None] * inv_freq
    return jnp.cos(ang), jnp.sin(ang)


def apply_rope(x, cos, sin):
    xf = x.astype(jnp.float32)
    x1, x2 = jnp.split(xf, 2, axis=-1)
    out = jnp.concatenate([x1 * cos - x2 * sin, x1 * sin + x2 * cos], axis=-1)
    return out.astype(x.dtype)


def causal_dwconv(x, w, b=None):
    s = x.shape[1]
    xp = jnp.pad(x, ((0, 0), (CONV_K - 1, 0), (0, 0)))
    y = xp[:, 0:s] * w[0]
    for j in range(1, CONV_K):
        y = y + xp[:, j:j + s] * w[j]
    if b is not None:
        y = y + b
    return y


def blocked_causal_attention(q, k, v, log_decay_cum=None):
    b, s, h, dk = q.shape
    nb = s // BLOCK
    scale = dk ** -0.5
    q_blocks = q.reshape(b, nb, BLOCK, h, dk).swapaxes(0, 1)
    starts = jnp.arange(nb, dtype=jnp.int32) * BLOCK
    k_pos = jnp.arange(s, dtype=jnp.int32)
    if log_decay_cum is None:
        xs = (q_blocks, starts)
    else:
        c_blocks = log_decay_cum.reshape(b, h, nb, BLOCK).transpose(2, 0, 1, 3)
        xs = (q_blocks, starts, c_blocks)

    def attend(blk):
        qi, start = blk[0], blk[1]
        sc = jnp.einsum('bqhd,bkhd->bhqk', qi, k).astype(jnp.float32) * scale
        if log_decay_cum is not None:
            sc = sc + blk[2][..., :, None] - log_decay_cum[:, :, None, :]
        q_pos = start + jnp.arange(BLOCK, dtype=jnp.int32)
        mask = k_pos[None, :] <= q_pos[:, None]
        sc = jnp.where(mask, sc, -jnp.inf)
        pr = jax.nn.softmax(sc, axis=-1).astype(v.dtype)
        return jnp.einsum('bhqk,bkhd->bqhd', pr, v)

    o = lax.map(attend, xs)
    return o.swapaxes(0, 1).reshape(b, s, h, v.shape[-1])


def mla_mixer(zq, zkv, zr, cos, sin, q_norm, w_uq, kv_norm, w_ukv):
    b, s, _ = zq.shape
    q = (rmsnorm(zq, q_norm) @ w_uq).reshape(b, s, MLA_HEADS, MLA_NOPE + MLA_ROPE)
    q_nope, q_rope = q[..., :MLA_NOPE], q[..., MLA_NOPE:]
    q_rope = apply_rope(q_rope, cos[:, :, None, :], sin[:, :, None, :])
    kv = (rmsnorm(zkv, kv_norm) @ w_ukv).reshape(b, s, MLA_HEADS, MLA_NOPE + MLA_V)
    k_nope, v = kv[..., :MLA_NOPE], kv[..., MLA_NOPE:]
    k_rope = apply_rope(zr, cos, sin)
    k = jnp.concatenate([k_nope, jnp.broadcast_to(k_rope[:, :, None, :], (b, s, MLA_HEADS, MLA_ROPE))], axis=-1)
    q = jnp.concatenate([q_nope, q_rope], axis=-1)
    return blocked_causal_attention(q, k, v).reshape(b, s, MLA_WIDTH)


def conv_mixer(zb, zc, zh, conv_w):
    return zb * causal_dwconv(zc * zh, conv_w)


def fox_mixer(fq, fk, fv, ff, b_forget):
    b, s, _ = fq.shape
    q = fq.reshape(b, s, FOX_HEADS, FOX_DIM)
    k = fk.reshape(b, s, FOX_HEADS, FOX_DIM)
    v = fv.reshape(b, s, FOX_HEADS, FOX_DIM)
    log_f = jax.nn.log_sigmoid(ff.astype(jnp.float32) + b_forget.astype(jnp.float32))
    cum = jnp.cumsum(log_f, axis=1).transpose(0, 2, 1)
    return blocked_causal_attention(q, k, v, cum).reshape(b, s, FOX_WIDTH)


def conv_glu_ffn(m, w_up, conv_w, conv_b, w_down):
    u = causal_dwconv(m @ w_up, conv_w, conv_b)
    g, val = jnp.split(u, 2, axis=-1)
    return (jax.nn.silu(g) * val) @ w_down


def setup_inputs(seed: int = 0) -> dict:
    key = jax.random.key(seed)
    ks = jax.random.split(key, 24)
    f32 = jnp.float32

    def nrm(k, shape, fan_in):
        return jax.random.normal(k, shape, f32) * (fan_in ** -0.5)

    def gain(k, shape):
        return 1.0 + 0.05 * jax.random.normal(k, shape, f32)

    x = jax.random.normal(ks[0], (BATCH, SEQ, D_MODEL), f32)
    p = jax.random.normal(ks[1], (DEPTH, BATCH, SEQ, PLE_DIM), f32)
    offsets = jax.random.randint(ks[2], (BATCH, 1), 0, 1024, dtype=jnp.int32)
    positions = offsets + jnp.arange(SEQ, dtype=jnp.int32)[None, :]
    return {
        "x": x,
        "p": p,
        "positions": positions,
        "attn_norm": gain(ks[3], (DEPTH, D_MODEL)),
        "w_in": nrm(ks[4], (DEPTH, D_MODEL, N_IN), D_MODEL),
        "b_forget": 2.0 + 0.1 * jax.random.normal(ks[5], (DEPTH, FOX_HEADS), f32),
        "q_norm": gain(ks[6], (DEPTH, Q_RANK)),
        "w_uq": nrm(ks[7], (DEPTH, Q_RANK, MLA_HEADS * (MLA_NOPE + MLA_ROPE)), Q_RANK),
        "kv_norm": gain(ks[8], (DEPTH, KV_RANK)),
        "w_ukv": nrm(ks[9], (DEPTH, KV_RANK, MLA_HEADS * (MLA_NOPE + MLA_V)), KV_RANK),
        "conv_w": nrm(ks[10], (DEPTH, CONV_K, CONV_WIDTH), CONV_K),
        "mla_out_norm": gain(ks[11], (DEPTH, MLA_WIDTH)),
        "conv_out_norm": gain(ks[12], (DEPTH, CONV_WIDTH)),
        "fox_out_norm": gain(ks[13], (DEPTH, FOX_WIDTH)),
        "w_out": nrm(ks[14], (DEPTH, D_MIX, D_MODEL), D_MIX),
        "ffn_norm": gain(ks[15], (DEPTH, D_MODEL)),
        "w_up": nrm(ks[16], (DEPTH, D_MODEL, 2 * D_FF), D_MODEL),
        "ffn_conv_w": nrm(ks[17], (DEPTH, CONV_K, 2 * D_FF), CONV_K),
        "ffn_conv_b": 0.02 * jax.random.normal(ks[18], (DEPTH, 2 * D_FF), f32),
        "w_down": nrm(ks[19], (DEPTH, D_FF, D_MODEL), D_FF),
        "ple_norm": gain(ks[20], (DEPTH, D_MODEL)),
        "w_ple_gate": nrm(ks[21], (DEPTH, D_MODEL, D_MODEL), D_MODEL),
        "w_ple": nrm(ks[22], (DEPTH, PLE_DIM, D_MODEL), PLE_DIM),
        "final_norm": gain(ks[23], (D_MODEL,)),
    }


def reference(x, p, positions, attn_norm, w_in, b_forget, q_norm, w_uq, kv_norm, w_ukv,
              conv_w, mla_out_norm, conv_out_norm, fox_out_norm, w_out, ffn_norm, w_up,
              ffn_conv_w, ffn_conv_b, w_down, ple_norm, w_ple_gate, w_ple, final_norm):
    cos, sin = rope_tables(positions)
    h = x
    for i in range(DEPTH):
        a = rmsnorm(h, attn_norm[i])
        z = a @ w_in[i]
        zq, zkv, zr, zb, zc, zh, fq, fk, fv, ff = jnp.split(z, IN_SPLITS, axis=-1)
        o_mla = mla_mixer(zq, zkv, zr, cos, sin, q_norm[i], w_uq[i], kv_norm[i], w_ukv[i])
        o_conv = conv_mixer(zb, zc, zh, conv_w[i])
        o_fox = fox_mixer(fq, fk, fv, ff, b_forget[i])
        mixed = jnp.concatenate([rmsnorm(o_mla, mla_out_norm[i]),
                                 rmsnorm(o_conv, conv_out_norm[i]),
                                 rmsnorm(o_fox, fox_out_norm[i])], axis=-1)
        h = h + mixed @ w_out[i]
        m = rmsnorm(h, ffn_norm[i])
        h = h + conv_glu_ffn(m, w_up[i], ffn_conv_w[i], ffn_conv_b[i], w_down[i])
        gate = jax.nn.sigmoid(rmsnorm(h, ple_norm[i]) @ w_ple_gate[i])
        h = h + gate * (p[i] @ w_ple[i])
    return rmsnorm(h, final_norm)
```

```python
import math
import contextlib
import numpy as np
import concourse.bass as bass
import concourse.mybir as mybir
from concourse.bass_utils import run_bass_kernel_spmd

F32 = mybir.dt.float32
BF16 = mybir.dt.bfloat16
I32 = mybir.dt.int32
AF = mybir.ActivationFunctionType
ALU = mybir.AluOpType

D = 1024
NH = 6
DFF = 2816
NFC = 22
SEQ = 4096
DEPTH = 2
EPS = 1e-6

NWIN = 2504
C_ZQ, C_ZKV, C_ZRA, C_ZRB, C_ZB, C_ZC, C_ZH, C_FQ, C_FK, C_FV, C_FF = 0, 256, 384, 480, 576, 832, 1088, 1344, 1728, 2112, 2496
OFF_WIN = 0
OFF_WUQ = OFF_WIN + 8 * NWIN
OFF_WUKV = OFF_WUQ + 2 * 1152
OFF_WOUT = OFF_WUKV + 768
OFF_WUP = OFF_WOUT + 8 * 1024
OFF_WDN = OFF_WUP + 8 * 5632
OFF_WPG = OFF_WDN + 8 * 2816
OFF_WPLE = OFF_WPG + 8 * 1024
TOT = OFF_WPLE + 2 * 1024
GA, GQ, GKV, GO, GF, GP, NG = 0, 8, 10, 11, 19, 27, 35
SV_CW, SV_FW, SV_FB, SV_BF, SV_FN, NSV = 0, 6, 138, 182, 183, 191
CS_ID, CS_TRI, CS_IF, NCST = 0, 128, 256, 258

NDS = 24


def _weight_image(i, w):
    img = np.zeros((128, TOT), np.float32)
    w_in = w["w_in"][i]
    for kc in range(8):
        r = w_in[kc * 128:(kc + 1) * 128]
        b = OFF_WIN + kc * NWIN
        img[:, b + C_ZQ:b + C_ZQ + 384] = r[:, 0:384]
        img[:, b + C_ZRA + 64:b + C_ZRA + 96] = r[:, 384:416]
        img[:, b + C_ZRB + 64:b + C_ZRB + 80] = r[:, 400:416]
        img[:, b + C_ZRB + 80:b + C_ZRB + 96] = r[:, 384:400]
        img[:, b + C_ZB:b + C_ZB + 1926] = r[:, 416:2342]
    w_uq = w["w_uq"][i]
    for kc in range(2):
        rh = w_uq[kc * 128:(kc + 1) * 128].reshape(128, NH, 96)
        b = OFF_WUQ + kc * 1152
        img[:, b:b + 384] = rh[:, :, 0:64].reshape(128, 384)
        img[:, b + 384:b + 576] = rh[:, :, 64:96].reshape(128, 192)
        rb = np.concatenate([rh[:, :, 80:96], rh[:, :, 64:80]], axis=2)
        img[:, b + 576:b + 768] = rb.reshape(128, 192)
    w_ukv = w["w_ukv"][i].reshape(128, NH, 2, 64)
    img[:, OFF_WUKV:OFF_WUKV + 384] = w_ukv[:, :, 0, :].reshape(128, 384)
    img[:, OFF_WUKV + 384:OFF_WUKV + 768] = w_ukv[:, :, 1, :].reshape(128, 384)
    img[:, OFF_WOUT:OFF_WOUT + 8192] = w["w_out"][i].reshape(8, 128, 1024).transpose(1, 0, 2).reshape(128, 8192)
    w_up = w["w_up"][i].reshape(8, 128, 2, NFC, 128)
    img[:, OFF_WUP:OFF_WUP + 8 * 5632] = w_up.transpose(1, 0, 3, 2, 4).reshape(128, 8 * 5632)
    w_dn = w["w_down"][i].reshape(NFC, 128, 8, 128)
    img[:, OFF_WDN:OFF_WDN + 8 * 2816] = w_dn.transpose(1, 2, 0, 3).reshape(128, 8 * 2816)
    img[:, OFF_WPG:OFF_WPG + 8192] = w["w_ple_gate"][i].reshape(8, 128, 1024).transpose(1, 0, 2).reshape(128, 8192)
    img[:, OFF_WPLE:OFF_WPLE + 2048] = w["w_ple"][i].reshape(2, 128, 1024).transpose(1, 0, 2).reshape(128, 2048)
    g = np.zeros((128, NG), np.float32)
    g[:, GA:GA + 8] = w["attn_norm"][i].reshape(8, 128).T
    g[:, GQ:GQ + 2] = w["q_norm"][i].reshape(2, 128).T
    g[:, GKV] = w["kv_norm"][i]
    on = np.concatenate([w["mla_out_norm"][i], w["conv_out_norm"][i], w["fox_out_norm"][i]])
    g[:, GO:GO + 8] = on.reshape(8, 128).T
    g[:, GF:GF + 8] = w["ffn_norm"][i].reshape(8, 128).T
    g[:, GP:GP + 8] = w["ple_norm"][i].reshape(8, 128).T
    sv = np.zeros((128, NSV), np.float32)
    sv[:, SV_CW:SV_CW + 6] = w["conv_w"][i].reshape(3, 2, 128).transpose(2, 0, 1).reshape(128, 6)
    fw = w["ffn_conv_w"][i].reshape(3, 2, NFC, 128)
    sv[:, SV_FW:SV_FW + 132] = fw.transpose(3, 0, 2, 1).reshape(128, 132)
    fb = w["ffn_conv_b"][i].reshape(2, NFC, 128)
    sv[:, SV_FB:SV_FB + 44] = fb.transpose(2, 1, 0).reshape(128, 44)
    sv[0:NH, SV_BF] = w["b_forget"][i]
    sv[:, SV_FN:SV_FN + 8] = w["final_norm"].reshape(8, 128).T
    return img, g, sv


def _consts():
    c = np.zeros((128, NCST), np.float32)
    c[:, CS_ID:CS_ID + 128] = np.eye(128, dtype=np.float32)
    c[:, CS_TRI:CS_TRI + 128] = np.triu(np.ones((128, 128), np.float32))
    inv = (10000.0 ** (-np.arange(0, 32, 2, dtype=np.float32) / 32)).astype(np.float32)
    for p in range(64, 96):
        c[p, CS_IF] = inv[(p - 64) % 16]
    return c


class Buf:
    __slots__ = ("w", "r")

    def __init__(self):
        self.w = None
        self.r = {}


class Eng:
    def __init__(self, name, eng, sem, sid):
        self.name, self.e, self.sem, self.sid = name, eng, sem, sid
        self.cnt = 0
        self.waited = {}


class KC:
    def __init__(self, nc, es):
        self.nc, self.es = nc, es
        self.semh = {}
        self.E = {}
        sid = 0
        for n, e in (("pe", nc.tensor), ("act", nc.scalar), ("dve", nc.vector), ("pool", nc.gpsimd), ("sp", nc.sync)):
            h = es.enter_context(nc.semaphore("s_" + n))
            self.semh[sid] = h
            self.E[n] = Eng(n, e, h, sid)
            sid += 1
        self.dsem = {"sp": [], "pool": []}
        for q in ("sp", "pool"):
            for i in range(NDS // 2):
                h = es.enter_context(nc.semaphore("d%s%d" % (q, i)))
                self.semh[sid] = h
                self.dsem[q].append([sid, 0])
                sid += 1
        self.dnext = {"sp": 0, "pool": 0}
        self.uid = 0

    @staticmethod
    def _flat(bs):
        out = []
        for b in bs:
            if isinstance(b, (list, tuple)):
                out.extend(KC._flat(b))
            else:
                out.append(b)
        return out

    def _deps(self, reads, writes):
        reads, writes = self._flat(reads), self._flat(writes)
        d = {}
        for b in reads:
            if b.w is not None:
                s, v = b.w
                if d.get(s, 0) < v:
                    d[s] = v
        for b in writes:
            if b.w is not None:
                s, v = b.w
                if d.get(s, 0) < v:
                    d[s] = v
            for s, v in b.r.items():
                if d.get(s, 0) < v:
                    d[s] = v
        return d

    def _wait(self, E, deps, skip_self=False):
        for s, v in deps.items():
            if skip_self and s == E.sid:
                continue
            if E.waited.get(s, 0) < v:
                E.e.wait_ge(self.semh[s], v)
                E.waited[s] = v

    def _mark(self, ev, reads, writes):
        reads, writes = self._flat(reads), self._flat(writes)
        s, v = ev
        for b in reads:
            if b.r.get(s, 0) < v:
                b.r[s] = v
        for b in writes:
            b.w = ev
            b.r = {}

    def op(self, en, fn, reads=(), writes=()):
        E = self.E[en]
        self._wait(E, self._deps(reads, writes), skip_self=(en == "pe"))
        ins = fn(E.e)
        E.cnt += 1
        ins.then_inc(E.sem, 1)
        self._mark((E.sid, E.cnt), reads, writes)

    def mm(self, out, pairs, reads=(), writes=(), start=True, stop=True):
        E = self.E["pe"]
        self._wait(E, self._deps(reads, writes), skip_self=True)
        n = len(pairs)
        ins = None
        for i, (l, r) in enumerate(pairs):
            ins = E.e.matmul(out, lhsT=l, rhs=r, start=(start and i == 0), stop=(stop and i == n - 1))
        E.cnt += 1
        ins.then_inc(E.sem, 1)
        self._mark((E.sid, E.cnt), reads, writes)

    def dma(self, qn, out, in_, reads=(), writes=()):
        E = self.E[qn]
        deps = self._deps(reads, writes)
        slot = self.dsem[qn][self.dnext[qn]]
        self.dnext[qn] = (self.dnext[qn] + 1) % (NDS // 2)
        sid, cnt = slot
        if cnt > 0 and deps.get(sid, 0) < 16 * cnt:
            deps[sid] = 16 * cnt
        self._wait(E, deps)
        ins = E.e.dma_start(out=out, in_=in_)
        slot[1] = cnt + 1
        ins.then_inc(self.semh[sid], 16)
        self._mark((sid, 16 * (cnt + 1)), reads, writes)

    def fence(self, engines=("pe", "act", "dve", "pool", "sp")):
        cur = {}
        for E in self.E.values():
            if E.cnt > 0:
                cur[E.sid] = E.cnt
        for q in self.dsem.values():
            for sid, cnt in q:
                if cnt > 0:
                    cur[sid] = 16 * cnt
        for en in engines:
            self._wait(self.E[en], dict(cur))


class Ring:
    def __init__(self, C, es, name, shape, dt, n, psum=False):
        self.t = []
        for i in range(n):
            C.uid += 1
            nm = "%s_%d_%d" % (name, i, C.uid)
            if psum:
                self.t.append(es.enter_context(C.nc.psum_tensor(nm, list(shape), dt)))
            else:
                self.t.append(es.enter_context(C.nc.sbuf_tensor(nm, list(shape), dt)))
        self.b = [Buf() for _ in range(n)]
        self.i = 0
        self.n = n

    def next(self):
        k = self.i
        self.i = (self.i + 1) % self.n
        return self.t[k], self.b[k]


def tile(C, es, name, shape, dt):
    C.uid += 1
    return es.enter_context(C.nc.sbuf_tensor("%s_%d" % (name, C.uid), list(shape), dt)), Buf()


def build(S=SEQ, L=DEPTH, dbg=False):
    assert S % 512 == 0
    NG5 = S // 512
    NT = S // 128
    nc = bass.Bass("TRN2", target_bir_lowering=False)
    okind = "ExternalOutput" if dbg else "Internal"
    xT = nc.dram_tensor("xT", [D, S], F32, kind="ExternalInput").ap()
    pT = nc.dram_tensor("pT", [L, 256, S], F32, kind="ExternalInput").ap()
    pos = nc.dram_tensor("pos", [1, S], I32, kind="ExternalInput").ap()
    wimg = nc.dram_tensor("wimg", [L, 128, TOT], F32, kind="ExternalInput").ap()
    gimg = nc.dram_tensor("gimg", [L, 128, NG], F32, kind="ExternalInput").ap()
    svimg = nc.dram_tensor("svimg", [L, 128, NSV], F32, kind="ExternalInput").ap()
    cimg = nc.dram_tensor("cimg", [128, NCST], F32, kind="ExternalInput").ap()
    outT = nc.dram_tensor("outT", [D, S], F32, kind="ExternalOutput").ap()
    WB = nc.dram_tensor("WB", [L, 128, TOT], BF16).ap()
    ROPE = nc.dram_tensor("ROPE", [2, 32, S], F32, kind=okind).ap()
    HT = nc.dram_tensor("HT", [D, S], F32, kind=okind).ap()
    MIX = nc.dram_tensor("MIX", [D, S], F32, kind=okind).ap()
    QM = nc.dram_tensor("QM", [NH, 96, S], BF16, kind=okind).ap()
    KM = nc.dram_tensor("KM", [NH, 64, S], BF16, kind=okind).ap()
    KR = nc.dram_tensor("KR", [32, S], BF16, kind=okind).ap()
    VM = nc.dram_tensor("VM", [128, NT, NH, 128], BF16, kind=okind).ap()
    QF = nc.dram_tensor("QF", [NH, 64, S], BF16, kind=okind).ap()
    KF = nc.dram_tensor("KF", [NH, 64, S], BF16, kind=okind).ap()
    VF = nc.dram_tensor("VF", [128, NT, NH, 128], BF16, kind=okind).ap()
    CA = nc.dram_tensor("CA", [NH, 6, S], BF16, kind=okind).ap()
    bWB = [Buf() for _ in range(L)]
    RELAY = nc.dram_tensor("RELAY", [1, 16], BF16).ap()
    bROPE, bHT, bMIX = Buf(), Buf(), Buf()
    bQM, bKM, bKR, bVM, bQF, bKF, bVF, bCA, bNCM = (Buf() for _ in range(9))
    bQMr = [Buf() for _ in range(NH)]

    top = contextlib.ExitStack()
    with top:
        C = KC(nc, top)
        cst, bcst = tile(C, top, "cst", [128, NCST], F32)
        gt, bgt = tile(C, top, "gt", [128, L, NG], F32)
        svt, bsvt = tile(C, top, "svt", [128, L, NSV], F32)
        nbf, bnbf = tile(C, top, "nbf", [128, L], F32)
        onesb, bones = tile(C, top, "onesb", [128, 128], BF16)
        trib, btri = tile(C, top, "trib", [128, 128], BF16)
        id6, bid6 = cst, bcst
        C.dma("sp", cst[:], cimg[:, :], writes=[bcst])
        for l in range(L):
            C.dma("sp", gt[:, l, :], gimg[l], writes=[bgt])
            C.dma("sp", svt[:, l, :], svimg[l], writes=[bsvt])
        C.op("dve", lambda e: e.memset(onesb[:], 1.0), writes=[bones])
        C.op("dve", lambda e: e.tensor_copy(out=trib[:], in_=cst[:, CS_TRI:CS_TRI + 128]), reads=[bcst], writes=[btri])
        identb, bidb = tile(C, top, "identb", [128, 128], BF16)
        negm, bnegm = tile(C, top, "negm", [128, 128], BF16)
        C.op("dve", lambda e: e.tensor_copy(out=identb[:], in_=cst[:, CS_ID:CS_ID + 128]), reads=[bcst], writes=[bidb])
        C.op("dve", lambda e: e.tensor_scalar(out=negm[:], in0=cst[:, CS_TRI:CS_TRI + 128], scalar1=-1.0, scalar2=29952.0, op0=ALU.add, op1=ALU.mult),
             reads=[bcst], writes=[bnegm])
        for l in range(L):
            C.op("dve", lambda e: e.tensor_scalar(out=nbf[:, l:l + 1], in0=svt[:, l, SV_BF:SV_BF + 1], scalar1=-1.0, scalar2=None, op0=ALU.mult),
                 reads=[bsvt], writes=[bnbf])

        epst, beps = tile(C, top, "epst", [128, 2], F32)
        C.op("dve", lambda e: e.memset(epst[:, 0:1], EPS), writes=[beps])
        C.op("dve", lambda e: e.memset(epst[:, 1:2], 1.0), writes=[beps])
        es0 = contextlib.ExitStack()
        win0, bwin0 = tile(C, es0, "win0", [128, 8, NWIN], BF16)
        wuq0, bwuq0 = tile(C, es0, "wuq0", [128, 2, 1152], BF16)
        wukv0, bwukv0 = tile(C, es0, "wukv0", [128, 768], BF16)
        pieces = []
        for kc in range(8):
            pieces += [(OFF_WIN + kc * NWIN, 1252, GA + kc), (OFF_WIN + kc * NWIN + 1252, 1252, GA + kc)]
        for kc in range(2):
            pieces.append((OFF_WUQ + kc * 1152, 1152, GQ + kc))
        pieces.append((OFF_WUKV, 768, GKV))
        NP1 = len(pieces)
        for kc in range(8):
            pieces.append((OFF_WOUT + kc * 1024, 1024, GO + kc))
        for kc in range(8):
            for q in range(4):
                pieces.append((OFF_WUP + kc * 5632 + q * 1408, 1408, GF + kc))
        for q in range(11):
            pieces.append((OFF_WDN + q * 2048, 2048, None))
        for kc in range(8):
            pieces.append((OFF_WPG + kc * 1024, 1024, GP + kc))
        pieces.append((OFF_WPLE, 2048, None))
        negs = []
        for kc in range(8):
            negs.append(OFF_WIN + kc * NWIN + C_ZRB + 64)
        for kc in range(2):
            for h in range(NH):
                negs.append(OFF_WUQ + kc * 1152 + 576 + h * 32)
        bW = [[Buf() for _ in pieces] for _ in range(L)]

        def wbufs(l, lo, hi):
            return [bW[l][i] for i, (c0, n, gc) in enumerate(pieces) if c0 < hi and c0 + n > lo]

        def prep_gen(items, sf, sb, engines):
            pend = None
            for cnt, (l, pi) in enumerate(items):
                c0, n, gc = pieces[pi]
                f, bf_ = sf.next()
                b, bb = sb.next()
                C.dma("sp", f[:, 0:n], wimg[l, :, c0:c0 + n], writes=[bf_])
                en = engines[cnt % len(engines)]
                if gc is None:
                    if en == "act":
                        C.op("act", lambda e: e.copy(out=b[:, 0:n], in_=f[:, 0:n]), reads=[bf_], writes=[bb])
                    else:
                        C.op(en, lambda e: e.tensor_copy(out=b[:, 0:n], in_=f[:, 0:n]), reads=[bf_], writes=[bb])
                else:
                    if en == "act":
                        C.op("act", lambda e: e.mul(out=b[:, 0:n], in_=f[:, 0:n], mul=gt[:, l, gc:gc + 1]), reads=[bf_, bgt], writes=[bb])
                    else:
                        C.op(en, lambda e: e.tensor_scalar(out=b[:, 0:n], in0=f[:, 0:n], scalar1=gt[:, l, gc:gc + 1], scalar2=None, op0=ALU.mult),
                             reads=[bf_, bgt], writes=[bb])
                for ng in negs:
                    if c0 <= ng < c0 + n:
                        o = ng - c0
                        C.op("dve", lambda e: e.tensor_scalar(out=b[:, o:o + 16], in0=b[:, o:o + 16], scalar1=-1.0, scalar2=None, op0=ALU.mult),
                             reads=[bb], writes=[bb])
                if pend is not None:
                    C.dma("pool", *pend[0], reads=pend[1], writes=pend[2])
                pend = ((WB[l, :, c0:c0 + n], b[:, 0:n]), [bb], [bW[l][pi]])
                yield
            if pend is not None:
                C.dma("pool", *pend[0], reads=pend[1], writes=pend[2])
            yield

        with contextlib.ExitStack() as es:
            sf = Ring(C, es, "sf", [128, 2048], F32, 3)
            sb = Ring(C, es, "sb", [128, 2048], BF16, 3)
            for pi in range(NP1):
                c0, n, gc = pieces[pi]
                f, bf_ = sf.next()
                C.dma("sp", f[:, 0:n], wimg[0, :, c0:c0 + n], writes=[bf_])
                if c0 < OFF_WUQ:
                    kc_, off_ = divmod(c0 - OFF_WIN, NWIN)
                    dst, bd = win0[:, kc_, off_:off_ + n], bwin0
                elif c0 < OFF_WUKV:
                    dst, bd = wuq0[:, (c0 - OFF_WUQ) // 1152, 0:n], bwuq0
                else:
                    dst, bd = wukv0[:, 0:n], bwukv0
                C.op("act", lambda e: e.mul(out=dst, in_=f[:, 0:n], mul=gt[:, 0, gc:gc + 1]), reads=[bf_, bgt], writes=[bd])
                for ng in negs:
                    if c0 <= ng < c0 + n:
                        o = ng - c0
                        C.op("dve", lambda e: e.tensor_scalar(out=dst[:, o:o + 16], in0=dst[:, o:o + 16], scalar1=-1.0, scalar2=None, op0=ALU.mult),
                             reads=[bd], writes=[bd])
            pi_, bpi = tile(C, es, "pi", [128, S], I32)
            pf, bpf = tile(C, es, "pf", [128, S], F32)
            ang, bang = tile(C, es, "ang", [128, S], F32)
            kq, bkq = tile(C, es, "kq", [128, S], F32)
            ki, bki = tile(C, es, "ki", [128, S], I32)
            rr, brr = tile(C, es, "rr", [128, S], F32)
            sn, bsn = tile(C, es, "sn", [128, S], F32)
            R = slice(64, 96)
            C1 = 6.28125
            C2 = 2 * math.pi - C1
            C.dma("sp", pi_[R, :], pos[0:1, :].to_broadcast([32, S]), writes=[bpi])
            C.op("dve", lambda e: e.tensor_copy(out=pf[R, :], in_=pi_[R, :]), reads=[bpi], writes=[bpf])
            C.op("dve", lambda e: e.tensor_scalar(out=ang[R, :], in0=pf[R, :], scalar1=cst[R, CS_IF:CS_IF + 1], scalar2=None, op0=ALU.mult),
                 reads=[bpf, bcst], writes=[bang])
            for which in (0, 1):
                if which == 0:
                    C.op("dve", lambda e: e.tensor_scalar(out=pf[R, :], in0=ang[R, :], scalar1=math.pi / 2, scalar2=None, op0=ALU.add),
                         reads=[bang], writes=[bpf])
                    src, bsrc = pf, bpf
                else:
                    src, bsrc = ang, bang
                C.op("dve", lambda e: e.tensor_scalar(out=kq[R, :], in0=src[R, :], scalar1=1.0 / (2 * math.pi), scalar2=None, op0=ALU.mult),
                     reads=[bsrc], writes=[bkq])
                C.op("dve", lambda e: e.tensor_copy(out=ki[R, :], in_=kq[R, :]), reads=[bkq], writes=[bki])
                C.op("dve", lambda e: e.tensor_copy(out=kq[R, :], in_=ki[R, :]), reads=[bki], writes=[bkq])
                C.op("dve", lambda e: e.scalar_tensor_tensor(out=rr[R, :], in0=kq[R, :], scalar=-C1, in1=src[R, :], op0=ALU.mult, op1=ALU.add),
                     reads=[bkq, bsrc], writes=[brr])
                C.op("dve", lambda e: e.scalar_tensor_tensor(out=rr[R, :], in0=kq[R, :], scalar=-C2, in1=rr[R, :], op0=ALU.mult, op1=ALU.add),
                     reads=[bkq, brr], writes=[brr])
                C.op("dve", lambda e: e.tensor_scalar(out=rr[R, :], in0=rr[R, :], scalar1=-3.1415925, scalar2=3.1415925, op0=ALU.max, op1=ALU.min),
                     reads=[brr], writes=[brr])
                C.op("act", lambda e: e.activation(out=sn[R, :], in_=rr[R, :], func=AF.Sin), reads=[brr], writes=[bsn])
                C.dma("sp", ROPE[which], sn[R, :], reads=[bsn], writes=[bROPE])
            C.fence()

        def rms_bc(srcs, bsrcs, nfeat, sqr, pn, lnr, outr):
            pst, bps = pn.next()
            n = len(srcs)
            sqs = []
            for i, s_ in enumerate(srcs):
                sq, bsq = sqr.next()
                C.op("act", lambda e: e.activation(out=sq[:], in_=s_, func=AF.Square), reads=bsrcs, writes=[bsq])
                C.mm(pst[:], [(onesb[:], sq[:])], reads=[bsq, bones], writes=[bps], start=(i == 0), stop=(i == n - 1))
            ln_, bln = lnr.next()
            C.op("act", lambda e: e.activation(out=ln_[:], in_=pst[:], func=AF.Ln, bias=epst[:, 0:1], scale=1.0 / nfeat), reads=[bps, beps], writes=[bln])
            o, bo = outr.next()
            C.op("act", lambda e: e.activation(out=o[:], in_=ln_[:], func=AF.Exp, scale=-0.5), reads=[bln], writes=[bo])
            return o, bo


        hsrc, bhsrc = xT, Buf()
        for l in range(L):
            wb = WB[l]
            with contextlib.ExitStack() as es:
                if l == 0:
                    win, bwin, wuq, bwuq, wukv, bwukv = win0, bwin0, wuq0, bwuq0, wukv0, bwukv0
                    bwinA = bwinB = [bwin0]
                else:
                    win, bwin = tile(C, es, "win", [128, 8, NWIN], BF16)
                    wuq, bwuq = tile(C, es, "wuq", [128, 2, 1152], BF16)
                    wukv, bwukv = tile(C, es, "wukv", [128, 768], BF16)
                    bwinA = [Buf() for _ in range(8)]
                    bwinB = [Buf() for _ in range(8)]
                    for kc in range(8):
                        o = OFF_WIN + kc * NWIN
                        C.dma("sp", win[:, kc, 0:C_ZB], wb[:, o:o + C_ZB], reads=wbufs(l, o, o + C_ZB), writes=[bwinA[kc]])
                    for kc in range(8):
                        o = OFF_WIN + kc * NWIN
                        C.dma("sp", win[:, kc, C_ZB:NWIN], wb[:, o + C_ZB:o + NWIN], reads=wbufs(l, o + C_ZB, o + NWIN), writes=[bwinB[kc]])
                    C.dma("sp", wuq[:], wb[:, OFF_WUQ:OFF_WUQ + 2304].rearrange("p (c n) -> p c n", c=2), reads=wbufs(l, OFF_WUQ, OFF_WUQ + 2304), writes=[bwuq])
                    C.dma("sp", wukv[:], wb[:, OFF_WUKV:OFF_WUKV + 768], reads=wbufs(l, OFF_WUKV, OFF_WUKV + 768), writes=[bwukv])
                hr = Ring(C, es, "h32", [128, 8, 512], F32, 2)
                ar = Ring(C, es, "abf", [128, 8, 512], BF16, 1)
                ar.b = [[Buf() for _ in range(8)] for _ in range(ar.n)]
                sqr = Ring(C, es, "sq", [128, 512], BF16, 3)
                lnr = Ring(C, es, "ln", [128, 512], F32, 2)
                rsr = Ring(C, es, "rstd", [128, 512], F32, 3)
                pn = Ring(C, es, "pn", [128, 512], F32, 2, psum=True)
                pz = Ring(C, es, "pz", [128, 512], F32, 6, psum=True)
                zq_sb, bzq = tile(C, es, "zq_sb", [128, 2, 512], F32)
                zqn, bzqn = tile(C, es, "zqn", [128, 2, 512], BF16)
                zkv_sb, bzkv = tile(C, es, "zkv_sb", [128, 512], F32)
                zkvn, bzkvn = tile(C, es, "zkvn", [128, 512], BF16)
                roper = Ring(C, es, "rope", [128, 2, 512], F32, 2)
                roper.b = [[Buf() for _ in range(4)] for _ in range(roper.n)]
                qmr = Ring(C, es, "qms", [64, NH, 512], BF16, 1)
                ror = Ring(C, es, "ro", [128, 512], BF16, 2)
                kmr = Ring(C, es, "kms", [64, NH, 512], BF16, 1)
                krr = Ring(C, es, "krs", [32, 512], BF16, 1)
                vmr = Ring(C, es, "vms", [128, 4, NH, 128], BF16, 1)
                qfr = Ring(C, es, "qfs", [64, NH, 512], BF16, 1)
                kfr = Ring(C, es, "kfs", [64, NH, 512], BF16, 1)
                vfr = Ring(C, es, "vfs", [128, 4, NH, 128], BF16, 1)
                tar = Ring(C, es, "ta", [128, 512], F32, 2)
                tbr = Ring(C, es, "tb", [128, 512], F32, 2)
                zh_sb, bzh = tile(C, es, "zh_sb", [128, 2, 512], F32)
                xs, bxs = tile(C, es, "xs", [128, 2, 514], F32)
                t1, bt1 = tile(C, es, "t1", [128, 2, 512], F32)
                ocr = Ring(C, es, "ocs", [128, 2, 512], F32, 1)
                ex, bex = tile(C, es, "ex", [8, 512], F32)
                ncum, bncum = tile(C, es, "ncum", [8, 512], F32)
                carry, bcarry = tile(C, es, "carry", [8, 1], F32)
                one6, bone6 = tile(C, es, "one6", [8, 512], F32)
                c8, bc8 = ex, bex
                r1, br1 = tile(C, es, "r1", [8, 512], F32)
                car = Ring(C, es, "cas", [8, 6, 512], BF16, 1)
                for rg in (vmr, vfr):
                    for k in range(rg.n):
                        C.op("pool", lambda e: e.memset(rg.t[k][:, :, :, 64:128], 1.0), writes=[rg.b[k]])
                C.op("dve", lambda e: e.memset(xs[:, :, 0:2], 0.0), writes=[bxs])
                C.op("dve", lambda e: e.memset(carry[:], 0.0), writes=[bcarry])
                C.op("dve", lambda e: e.memset(one6[:], 1.0), writes=[bone6])
                hview = hsrc.rearrange("(c p) t -> p c t", p=128)

                def evac(i, out, in_, reads, writes):
                    if i % 4 != 3:
                        C.op("act", lambda e: e.copy(out=out, in_=in_), reads=reads, writes=writes)
                    else:
                        C.op("dve", lambda e: e.tensor_copy(out=out, in_=in_), reads=reads, writes=writes)

                GS = {}

                def norm_load(g):
                    ts_ = slice(g * 512, (g + 1) * 512)
                    h, bh = hr.next()
                    C.dma("sp", h[:], hview[:, :, ts_], reads=[bhsrc], writes=[bh])
                    rope, brope = roper.next()
                    for rep in range(4):
                        C.dma("sp", rope[32 * rep:32 * rep + 32, :, :], ROPE[:, :, ts_].rearrange("w p t -> p w t"), reads=[bROPE], writes=[brope[rep]])
                    GS[g] = dict(h=h, bh=bh, rope=rope, brope=brope, ts=ts_)

                def norm_a(g):
                    st = GS[g]
                    h, bh = st["h"], st["bh"]
                    sqs = []
                    for c in range(8):
                        sq, bsq = (sq8 if c < 6 else sqr).next()
                        C.op("act", lambda e: e.activation(out=sq[:], in_=h[:, c, :], func=AF.Square), reads=[bh], writes=[bsq])
                        sqs.append((sq, bsq))
                    st["sqs"] = sqs

                def norm_b(g):
                    st = GS[g]
                    h, bh = st["h"], st["bh"]
                    pst, bps = pn.next()
                    for c, (sq, bsq) in enumerate(st["sqs"]):
                        C.mm(pst[:], [(onesb[:], sq[:])], reads=[bsq, bones], writes=[bps], start=(c == 0), stop=(c == 7))
                    ln_, bln = lnr.next()
                    C.op("act", lambda e: e.activation(out=ln_[:], in_=pst[:], func=AF.Ln, bias=epst[:, 0:1], scale=1.0 / 1024), reads=[bps, beps], writes=[bln])
                    rs, brs = rsr.next()
                    C.op("act", lambda e: e.activation(out=rs[:], in_=ln_[:], func=AF.Exp, scale=-0.5), reads=[bln], writes=[brs])
                    a, ba = ar.next()
                    st["a"], st["ba"], st["rs"], st["brs"] = a, ba, rs, brs

                def norm_c(g, chunks):
                    st = GS[g]
                    h, bh, a, ba, rs, brs = st["h"], st["bh"], st["a"], st["ba"], st["rs"], st["brs"]
                    for c in chunks:
                        C.op("dve", lambda e: e.tensor_tensor(out=a[:, c, :], in0=h[:, c, :], in1=rs[:], op=ALU.mult), reads=[bh, brs], writes=[ba[c]])

                def main(g):
                    st = GS[g]
                    a, ba, rope, brope, ts_ = st["a"], st["ba"], st["rope"], st["brope"], st["ts"]

                    def zmm(col, m):
                        ps, bps = pz.next()
                        C.mm(ps[0:m, :], [(win[:, kc, col:col + m], a[:, kc, :]) for kc in range(8)], reads=[bwinA if col + m <= C_ZB else bwinB, ba], writes=[bps])
                        return ps, bps

                    if g + 1 < NG5:
                        norm_load(g + 1)
                    for m in range(2):
                        ps, bps = zmm(C_ZQ + 128 * m, 128)
                        evac(0, zq_sb[:, m, :], ps[:], [bps], [bzq])
                    ps, bps = zmm(C_ZKV, 128)
                    evac(0, zkv_sb[:], ps[:], [bps], [bzkv])
                    sqq = []
                    for m in range(2):
                        sq, bsq = sqr.next()
                        C.op("act", lambda e: e.activation(out=sq[:], in_=zq_sb[:, m, :], func=AF.Square), reads=[bzq], writes=[bsq])
                        sqq.append((sq, bsq))
                    sqk, bsqk = sqr.next()
                    C.op("act", lambda e: e.activation(out=sqk[:], in_=zkv_sb[:], func=AF.Square), reads=[bzkv], writes=[bsqk])
                    oc, boc = ocr.next()
                    for c in range(2):
                        pzh, bpzh = zmm(C_ZH + 128 * c, 128)
                        C.op("act", lambda e: e.copy(out=zh_sb[:, c, :], in_=pzh[:]), reads=[bpzh], writes=[bzh])
                        pzc, bpzc = zmm(C_ZC + 128 * c, 128)
                        C.op("dve", lambda e: e.tensor_tensor(out=xs[:, c, 2:514], in0=pzc[:], in1=zh_sb[:, c, :], op=ALU.mult), reads=[bpzc, bzh, bxs], writes=[bxs])
                        cw = lambda j: svt[:, l, SV_CW + 2 * j + c:SV_CW + 2 * j + c + 1]
                        C.op("dve", lambda e: e.tensor_scalar(out=t1[:, c, :], in0=xs[:, c, 2:514], scalar1=cw(2), scalar2=None, op0=ALU.mult), reads=[bxs, bsvt], writes=[bt1])
                        C.op("dve", lambda e: e.scalar_tensor_tensor(out=t1[:, c, :], in0=xs[:, c, 1:513], scalar=cw(1), in1=t1[:, c, :], op0=ALU.mult, op1=ALU.add),
                             reads=[bxs, bsvt, bt1], writes=[bt1])
                        C.op("dve", lambda e: e.scalar_tensor_tensor(out=t1[:, c, :], in0=xs[:, c, 0:512], scalar=cw(0), in1=t1[:, c, :], op0=ALU.mult, op1=ALU.add),
                             reads=[bxs, bsvt, bt1], writes=[bt1])
                        pzb, bpzb = zmm(C_ZB + 128 * c, 128)
                        C.op("dve", lambda e: e.tensor_tensor(out=oc[:, c, :], in0=pzb[:], in1=t1[:, c, :], op=ALU.mult), reads=[bpzb, bt1], writes=[boc])
                        C.op("pool", lambda e: e.tensor_copy(out=xs[:, c, 0:2], in_=xs[:, c, 512:514]), reads=[bxs], writes=[bxs])
                    C.dma("pool", MIX.rearrange("(c p) t -> p c t", p=128)[:, 3:5, ts_], oc[:], reads=[boc], writes=[bMIX])
                    pst, bpsq = pn.next()
                    for m, (sq, bsq) in enumerate(sqq):
                        C.mm(pst[:], [(onesb[:], sq[:])], reads=[bsq, bones], writes=[bpsq], start=(m == 0), stop=(m == 1))
                    ln_, bln = lnr.next()
                    C.op("act", lambda e: e.activation(out=ln_[:], in_=pst[:], func=AF.Ln, bias=epst[:, 0:1], scale=1.0 / 256), reads=[bpsq, beps], writes=[bln])
                    rq, brq = rsr.next()
                    C.op("act", lambda e: e.activation(out=rq[:], in_=ln_[:], func=AF.Exp, scale=-0.5), reads=[bln], writes=[brq])
                    pst2, bpsk = pn.next()
                    C.mm(pst2[:], [(onesb[:], sqk[:])], reads=[bsqk, bones], writes=[bpsk])
                    ln2, bln2 = lnr.next()
                    C.op("act", lambda e: e.activation(out=ln2[:], in_=pst2[:], func=AF.Ln, bias=epst[:, 0:1], scale=1.0 / 128), reads=[bpsk, beps], writes=[bln2])
                    rk, brk = rsr.next()
                    C.op("act", lambda e: e.activation(out=rk[:], in_=ln2[:], func=AF.Exp, scale=-0.5), reads=[bln2], writes=[brk])
                    for m in range(2):
                        C.op("dve", lambda e: e.tensor_tensor(out=zqn[:, m, :], in0=zq_sb[:, m, :], in1=rq[:], op=ALU.mult), reads=[bzq, brq], writes=[bzqn])
                    C.op("dve", lambda e: e.tensor_tensor(out=zkvn[:], in0=zkv_sb[:], in1=rk[:], op=ALU.mult), reads=[bzkv, brk], writes=[bzkvn])
                    pa, bpa = zmm(C_ZRA + 64, 32)
                    pb, bpb = zmm(C_ZRB + 64, 32)
                    ta, bta = tar.next()
                    tb, btb = tbr.next()
                    krs, bkrs = krr.next()
                    C.op("dve", lambda e: e.tensor_tensor(out=ta[0:32, :], in0=pa[0:32, :], in1=rope[0:32, 0, :], op=ALU.mult), reads=[bpa, brope], writes=[bta])
                    C.op("dve", lambda e: e.tensor_tensor(out=tb[0:32, :], in0=pb[0:32, :], in1=rope[0:32, 1, :], op=ALU.mult), reads=[bpb, brope], writes=[btb])
                    C.op("pool", lambda e: e.tensor_tensor(out=krs[0:32, :], in0=ta[0:32, :], in1=tb[0:32, :], op=ALU.add), reads=[bta, btb], writes=[bkrs])
                    C.dma("pool", KR[:, ts_], krs[0:32, :], reads=[bkrs], writes=[bKR])
                    pff, bpff = zmm(C_FF, 6)
                    C.op("act", lambda e: e.activation(out=ex[0:6, :], in_=pff[0:6, :], func=AF.Exp, bias=nbf[0:6, l:l + 1], scale=-1.0), reads=[bpff, bnbf], writes=[bex])
                    C.op("act", lambda e: e.activation(out=ex[0:6, :], in_=ex[0:6, :], func=AF.Ln, bias=epst[0:6, 1:2], scale=1.0), reads=[bex, beps], writes=[bex])
                    C.op("dve", lambda e: e.tensor_tensor_scan(out=ncum[0:6, :], data0=one6[0:6, :], data1=ex[0:6, :], initial=carry[0:6, 0:1], op0=ALU.mult, op1=ALU.add),
                         reads=[bone6, bex, bcarry], writes=[bncum])
                    C.op("dve", lambda e: e.tensor_copy(out=carry[0:6, 0:1], in_=ncum[0:6, 511:512]), reads=[bncum], writes=[bcarry])
                    cas, bcas = car.next()
                    C.op("dve", lambda e: e.tensor_scalar(out=c8[0:6, :], in0=ncum[0:6, :], scalar1=-8.0, scalar2=None, op0=ALU.mult), reads=[bncum], writes=[bc8])
                    C.op("dve", lambda e: e.tensor_copy(out=cas[0:6, 0, :], in_=c8[0:6, :]), reads=[bc8], writes=[bcas])
                    C.op("dve", lambda e: e.tensor_tensor(out=r1[0:6, :], in0=c8[0:6, :], in1=cas[0:6, 0, :], op=ALU.subtract), reads=[bc8, bcas], writes=[br1])
                    C.op("dve", lambda e: e.tensor_copy(out=cas[0:6, 1, :], in_=r1[0:6, :]), reads=[br1], writes=[bcas])
                    C.op("dve", lambda e: e.tensor_tensor(out=c8[0:6, :], in0=r1[0:6, :], in1=cas[0:6, 1, :], op=ALU.subtract), reads=[br1, bcas, bc8], writes=[bc8])
                    C.op("dve", lambda e: e.tensor_copy(out=cas[0:6, 2, :], in_=c8[0:6, :]), reads=[bc8], writes=[bcas])
                    C.op("dve", lambda e: e.tensor_scalar(out=cas[0:6, 3:6, :], in0=cas[0:6, 0:3, :], scalar1=-1.0, scalar2=None, op0=ALU.mult), reads=[bcas], writes=[bcas])
                    C.dma("pool", CA[:, :, ts_], cas[0:6, :, :], reads=[bcas], writes=[bCA])
                    qfs, bqfs = qfr.next()
                    kfs, bkfs = kfr.next()
                    for hp in range(3):
                        pq, bpq = zmm(C_FQ + 128 * hp, 128)
                        evac(0, qfs[:, 2 * hp, :], pq[0:64, :], [bpq], [bqfs])
                        evac(1, qfs[:, 2 * hp + 1, :], pq[64:128, :], [bpq], [bqfs])
                        pk, bpk = zmm(C_FK + 128 * hp, 128)
                        evac(2, kfs[:, 2 * hp, :], pk[0:64, :], [bpk], [bkfs])
                        evac(3, kfs[:, 2 * hp + 1, :], pk[64:128, :], [bpk], [bkfs])
                    C.dma("pool", QF.rearrange("h p t -> p h t")[:, :, ts_], qfs[:], reads=[bqfs], writes=[bQF])
                    C.dma("pool", KF.rearrange("h p t -> p h t")[:, :, ts_], kfs[:], reads=[bkfs], writes=[bKF])
                    vfs, bvfs = vfr.next()
                    for tt in range(4):
                        pv, bpv = pz.next()
                        C.mm(pv[:, 0:384], [(a[:, kc, tt * 128:(tt + 1) * 128], win[:, kc, C_FV:C_FV + 384]) for kc in range(8)], reads=[bwinB, ba], writes=[bpv])
                        evac(tt, vfs[:, tt, :, 0:64], pv[:, 0:384].rearrange("p (h d) -> p h d", h=NH), [bpv], [bvfs])
                    g4 = st["ts"].start // 128
                    C.dma("pool", VF[:, g4:g4 + 4, :, :], vfs[:], reads=[bvfs], writes=[bVF])
                    if g + 1 < NG5:
                        norm_a(g + 1)
                    qms, bqms = qmr.next()
                    nxg = g + 1 < NG5
                    for hp in range(3):
                        ps, bps = pz.next()
                        C.mm(ps[:, :], [(wuq[:, kc, 128 * hp:128 * hp + 128], zqn[:, kc, :]) for kc in range(2)], reads=[bwuq, bzqn], writes=[bps])
                        C.op("act", lambda e: e.copy(out=qms[:, 2 * hp, :], in_=ps[0:64, :]), reads=[bps], writes=[bqms])
                        C.op("act", lambda e: e.copy(out=qms[:, 2 * hp + 1, :], in_=ps[64:128, :]), reads=[bps], writes=[bqms])
                        if nxg:
                            if hp == 0:
                                norm_b(g + 1)
                            else:
                                norm_c(g + 1, (2 * hp - 2, 2 * hp - 1))
                    C.dma("pool", QM.rearrange("h p t -> p h t")[0:64, :, ts_], qms[:], reads=[bqms], writes=[bQM])
                    for gi, (h0, nh_) in enumerate(((0, 4), (4, 2))):
                        m = 32 * nh_
                        pa, bpa = pz.next()
                        C.mm(pa[0:m, :], [(wuq[:, kc, 384 + 32 * h0:384 + 32 * h0 + m], zqn[:, kc, :]) for kc in range(2)], reads=[bwuq, bzqn], writes=[bpa])
                        pb, bpb = pz.next()
                        C.mm(pb[0:m, :], [(wuq[:, kc, 576 + 32 * h0:576 + 32 * h0 + m], zqn[:, kc, :]) for kc in range(2)], reads=[bwuq, bzqn], writes=[bpb])
                        ta, bta = tar.next()
                        tb, btb = tbr.next()
                        ro, bro = ror.next()
                        C.op("dve", lambda e: e.tensor_tensor(out=ta[0:m, :], in0=pa[0:m, :], in1=rope[0:m, 0, :], op=ALU.mult), reads=[bpa, brope], writes=[bta])
                        C.op("dve", lambda e: e.tensor_tensor(out=tb[0:m, :], in0=pb[0:m, :], in1=rope[0:m, 1, :], op=ALU.mult), reads=[bpb, brope], writes=[btb])
                        C.op("pool", lambda e: e.tensor_tensor(out=ro[0:m, :], in0=ta[0:m, :], in1=tb[0:m, :], op=ALU.add), reads=[bta, btb], writes=[bro])
                        for hq in range(nh_):
                            C.dma("pool", QM[h0 + hq, 64:96, ts_], ro[32 * hq:32 * hq + 32, :], reads=[bro], writes=[bQMr[h0 + hq]])
                        if nxg:
                            norm_c(g + 1, (4 + 2 * gi, 5 + 2 * gi))
                    kms, bkms = kmr.next()
                    for hp in range(3):
                        pk, bpk = pz.next()
                        C.mm(pk[:, :], [(wukv[:, hp * 128:(hp + 1) * 128], zkvn[:])], reads=[bwukv, bzkvn], writes=[bpk])
                        evac(0, kms[:, 2 * hp, :], pk[0:64, :], [bpk], [bkms])
                        evac(1, kms[:, 2 * hp + 1, :], pk[64:128, :], [bpk], [bkms])
                    C.dma("pool", KM.rearrange("h p t -> p h t")[:, :, ts_], kms[:], reads=[bkms], writes=[bKM])
                    vms, bvms = vmr.next()
                    for tt in range(4):
                        pv, bpv = pz.next()
                        C.mm(pv[:, 0:384], [(zkvn[:, tt * 128:(tt + 1) * 128], wukv[:, 384:768])], reads=[bwukv, bzkvn], writes=[bpv])
                        evac(tt, vms[:, tt, :, 0:64], pv[:, 0:384].rearrange("p (h d) -> p h d", h=NH), [bpv], [bvms])
                    C.dma("pool", VM[:, g4:g4 + 4, :, :], vms[:], reads=[bvms], writes=[bVM])
                    del GS[g]

                sq8 = Ring(C, es, "sq8", [128, 512], BF16, 6)
                norm_load(0)
                norm_a(0)
                norm_b(0)
                norm_c(0, range(8))
                for g in range(NG5):
                    main(g)
                C.fence()

            if l == 0:
                es0.close()
            es_w = contextlib.ExitStack()
            wout, bwout = tile(C, es_w, "wout", [128, 8, 1024], BF16)
            wpg, bwpg = tile(C, es_w, "wpg", [128, 8, 1024], BF16)
            wple, bwple = tile(C, es_w, "wple", [128, 2, 1024], BF16)
            with contextlib.ExitStack() as es:
                kr_ = {"m": Ring(C, es, "Km", [96, S], BF16, 2), "f": Ring(C, es, "Kf", [96, S], BF16, 2)}
                qr_ = {"m": Ring(C, es, "Qm", [96, S], BF16, 2), "f": Ring(C, es, "Qf", [96, S], BF16, 2)}
                vr_ = Ring(C, es, "V", [128, NT, 128], BF16, 2)
                NCH = 4
                CW = S // NCH
                for rg in (kr_["m"], kr_["f"], qr_["m"], qr_["f"], vr_):
                    rg.b = [[Buf() for _ in range(NCH)] for _ in range(rg.n)]
                pss = Ring(C, es, "pss", [128, 512], F32, 5, psum=True)
                pso = Ring(C, es, "pso", [128, 512], F32, 3, psum=True)
                pr = Ring(C, es, "pexp", [128, 512], BF16, 6)
                rcr = Ring(C, es, "rc", [64, 512], F32, 2)
                sf2 = Ring(C, es, "sf2", [128, 2048], F32, 3)
                sb2 = Ring(C, es, "sb2", [128, 2048], BF16, 3)
                items = [(l, pi) for pi in range(NP1, len(pieces))]
                if l + 1 < L:
                    items += [(l + 1, pi) for pi in range(NP1)]
                pgen = prep_gen(items, sf2, sb2, ("dve",))
                osr = Ring(C, es, "ost", [64, 512], F32, 3)
                for k in range(2):
                    C.op("pool", lambda e: e.memset(kr_["f"].t[k][64:70, :], 1.0), writes=[kr_["f"].b[k]])
                    C.op("pool", lambda e: e.memset(qr_["f"].t[k][64:70, :], 1.0), writes=[qr_["f"].b[k]])
                basem, bbm = tile(C, es, "basem", [128, 512], F32)
                basef, bbf = tile(C, es, "basef", [128, 512], F32)
                C.op("dve", lambda e: e.memset(basem[:], math.exp(96 ** -0.5)), writes=[bbm])
                C.op("dve", lambda e: e.memset(basef[:], math.exp(0.125)), writes=[bbf])
                ssr = Ring(C, es, "ssb", [128, 512], F32, 2)
                cnt_off = [0]
                blocks = []
                for mixer in ("m", "f"):
                    for hh in range(NH):
                        for j in range(NG5):
                            for i in range(4 * j + 4):
                                blocks.append((mixer, hh, j, i))
                heads = {}
                head_order = [(mx_, hh) for mx_ in ("m", "f") for hh in range(NH)]
                accs = {}
                inflight = {}

                def load_head(mixer, hh):
                    Kt, bK = kr_[mixer].next()
                    Qt, bQ = qr_[mixer].next()
                    Vt, bV = vr_.next()
                    nct, bnct = None, None
                    if mixer == "m" and not heads:
                        for c in range(NCH):
                            cs = slice(c * CW, (c + 1) * CW)
                            tsl = slice(c * (NT // NCH), (c + 1) * (NT // NCH))
                            C.dma("sp", Kt[0:64, cs], KM[hh][:, cs], reads=[bKM], writes=[bK[c]])
                            C.dma("sp", Kt[64:96, cs], KR[:, cs], reads=[bKR], writes=[bK[c]])
                            C.dma("sp", Qt[0:96, cs], QM[hh][:, cs], reads=[bQM, bQMr[hh]], writes=[bQ[c]])
                            C.dma("sp", Vt[:, tsl, :], VM[:, tsl, hh, :], reads=[bVM], writes=[bV[c]])
                    elif mixer == "m":
                        C.dma("sp", Kt[0:64, :], KM[hh], reads=[bKM], writes=[bK])
                        C.dma("sp", Kt[64:96, :], KR[:, :], reads=[bKR], writes=[bK])
                        C.dma("sp", Qt[0:96, :], QM[hh], reads=[bQM, bQMr[hh]], writes=[bQ])
                        C.dma("sp", Vt[:], VM[:, :, hh, :], reads=[bVM], writes=[bV])
                    else:
                        C.dma("sp", Kt[0:64, :], KF[hh], reads=[bKF], writes=[bK])
                        C.dma("sp", Qt[0:64, :], QF[hh], reads=[bQF], writes=[bQ])
                        C.dma("sp", Qt[64:67, :], CA[hh, 0:3, :], reads=[bCA], writes=[bQ])
                        C.dma("sp", Kt[67:70, :], CA[hh, 3:6, :], reads=[bCA], writes=[bK])
                        C.dma("sp", Vt[:], VF[:, :, hh, :], reads=[bVF], writes=[bV])
                    heads[(mixer, hh)] = (Kt, bK, Qt, bQ, Vt, bV, nct, bnct)

                def qk(n):
                    mixer, hh, j, i = blocks[n]
                    if (mixer, hh) not in heads:
                        load_head(mixer, hh)
                    Kt, bK, Qt, bQ, Vt, bV, nct, bnct = heads[(mixer, hh)]
                    Kd = 96 if mixer == "m" else 70
                    r = i - 4 * j
                    c0 = 128 * r if r > 0 else 0
                    sps, bs = pss.next()
                    if r >= 0:
                        C.mm(sps[:, c0:512], [(Kt[0:Kd, i * 128:(i + 1) * 128], Qt[0:Kd, j * 512 + c0:(j + 1) * 512])], reads=[bK[(i * 128) // CW], bQ[(j * 512) // CW:((j + 1) * 512 - 1) // CW + 1]], writes=[bs],
                             start=True, stop=False)
                        C.mm(sps[:, c0:c0 + 128], [(identb[:], negm[:])], reads=[bidb, bnegm], writes=[bs], start=False, stop=True)
                    else:
                        C.mm(sps[:, c0:512], [(Kt[0:Kd, i * 128:(i + 1) * 128], Qt[0:Kd, j * 512 + c0:(j + 1) * 512])], reads=[bK[(i * 128) // CW], bQ[(j * 512) // CW:((j + 1) * 512 - 1) // CW + 1]], writes=[bs])
                    inflight[n] = (sps, bs, c0, r)

                def rest(n):
                    mixer, hh, j, i = blocks[n]
                    Kt, bK, Qt, bQ, Vt, bV, nct, bnct = heads[(mixer, hh)]
                    sps, bs, c0, r = inflight.pop(n)
                    sc = 96 ** -0.5 if mixer == "m" else 0.125
                    row0 = 0 if mixer == "m" else 640
                    ntile = 4 * j + 4
                    if i == 0:
                        accs[(mixer, hh, j)] = pso.next()
                    oacc, bo = accs[(mixer, hh, j)]
                    pt, bp = pr.next()
                    use_pool = False
                    if r < 0:
                        cnt_off[0] += 1
                        use_pool = False
                    if use_pool:
                        ssb, bss = ssr.next()
                        bt_, bbt = (basem, bbm) if mixer == "m" else (basef, bbf)
                        C.op("dve", lambda e: e.tensor_copy(out=ssb[:], in_=sps[:]), reads=[bs], writes=[bss])
                        C.op("pool", lambda e: e.tensor_tensor(out=pt[:], in0=bt_[:], in1=ssb[:], op=ALU.pow), reads=[bss, bbt], writes=[bp])
                    else:
                        C.op("act", lambda e: e.activation(out=pt[:, c0:512], in_=sps[:, c0:512], func=AF.Exp, scale=sc), reads=[bs], writes=[bp])
                    C.mm(oacc[:, c0:512], [(Vt[:, i, :], pt[:, c0:512])], reads=[bV[(i * 128) // CW], bp], writes=[bo], start=(i == 0), stop=(i == ntile - 1))
                    if j == 0 and i == 0:
                        hi_ = head_order.index((mixer, hh))
                        if hi_ + 1 < len(head_order) and head_order[hi_ + 1] not in heads:
                            load_head(*head_order[hi_ + 1])
                    if i == ntile - 1:
                        del accs[(mixer, hh, j)]
                        rc, brc = rcr.next()
                        ost, bost = osr.next()
                        C.op("dve", lambda e: e.reciprocal(out=rc[0:64, :], in_=oacc[64:128, :]), reads=[bo], writes=[brc])
                        C.op("dve", lambda e: e.tensor_tensor(out=ost[0:64, :], in0=oacc[0:64, :], in1=rc[0:64, :], op=ALU.mult), reads=[bo, brc], writes=[bost])
                        C.dma("pool", MIX[row0 + hh * 64:row0 + (hh + 1) * 64, j * 512:(j + 1) * 512], ost[0:64, :], reads=[bost], writes=[bMIX])

                LA = 4
                NB = len(blocks)
                pstep = max(1, (NB - 8) // (len(items) + 1))
                for n in range(NB + LA):
                    if n < NB:
                        qk(n)
                    if n >= LA:
                        rest(n - LA)
                    if n % pstep == pstep - 1:
                        next(pgen, None)
                for _ in pgen:
                    pass
                C.dma("sp", wout[:], wb[:, OFF_WOUT:OFF_WOUT + 8192].rearrange("p (c n) -> p c n", c=8), reads=wbufs(l, OFF_WOUT, OFF_WOUT + 8192), writes=[bwout])
                C.dma("sp", wpg[:], wb[:, OFF_WPG:OFF_WPG + 8192].rearrange("p (c n) -> p c n", c=8), reads=wbufs(l, OFF_WPG, OFF_WPG + 8192), writes=[bwpg])
                C.dma("sp", wple[:], wb[:, OFF_WPLE:OFF_WPLE + 2048].rearrange("p (c n) -> p c n", c=2), reads=wbufs(l, OFF_WPLE, OFF_WPLE + 2048), writes=[bwple])
                C.fence()

            with contextlib.ExitStack() as es:
                mxr = Ring(C, es, "mx", [128, 8, 512], F32, 1)
                hr = Ring(C, es, "h", [128, 8, 512], F32, 2)
                nbr = Ring(C, es, "nb", [128, 8, 512], BF16, 2)
                nbr.b = [[Buf() for _ in range(8)] for _ in range(nbr.n)]
                actt, _ba = tile(C, es, "actt", [128, NFC, 512], BF16)
                bact = [Buf() for _ in range(NFC)]
                brelay = Buf()
                wupr = Ring(C, es, "wup", [128, 8, 256], BF16, 3)
                wdnr = Ring(C, es, "wdn", [128, NFC, 128], BF16, 3)
                ppr = Ring(C, es, "pp", [128, 2, 512], F32, 1)
                pbr = Ring(C, es, "pb", [128, 2, 512], BF16, 2)
                sqr = None
                lnr = Ring(C, es, "ln", [128, 512], F32, 1)
                rsr = Ring(C, es, "rstd", [128, 512], F32, 3)
                xr2 = Ring(C, es, "xgv", [128, 2, 514], F32, 2)
                xr2.b = [[Buf(), Buf()] for _ in range(xr2.n)]
                xhb2 = [Buf() for _ in range(xr2.n)]
                tr_ = {0: Ring(C, es, "tg", [128, 512], F32, 3), 1: Ring(C, es, "tv", [128, 512], F32, 3)}
                sgr = Ring(C, es, "sg", [128, 512], F32, 2)
                halo, _bh = tile(C, es, "halo", [128, 2 * NFC, 2], F32)
                bhalo = [Buf() for _ in range(2 * NFC)]
                pn = Ring(C, es, "pn", [128, 512], F32, 2, psum=True)
                pz = Ring(C, es, "pz", [128, 512], F32, 6, psum=True)
                C.op("dve", lambda e: e.memset(halo[:], 0.0), writes=bhalo)
                mixv = MIX.rearrange("(c p) t -> p c t", p=128)
                hview = hsrc.rearrange("(c p) t -> p c t", p=128)
                last = (l == L - 1)
                dst, bdst = (outT, Buf()) if last else (HT, bHT)
                dview = dst.rearrange("(c p) t -> p c t", p=128)
                wupv = wb[:, OFF_WUP:OFF_WUP + 8 * 5632].rearrange("p (c k n) -> p c k n", c=8, k=NFC)
                wdnv = wb[:, OFF_WDN:OFF_WDN + 8 * 2816].rearrange("p (d k n) -> p d k n", d=8, k=NFC)
                G = {}

                def scale_bf(dstt, bd, src, bs_, rs, brs, chunks):
                    for n_, c in enumerate(chunks):
                        en = "dve" if n_ % 3 != 2 else "pool"
                        C.op(en, lambda e: e.tensor_tensor(out=dstt[:, c, :], in0=src[:, c, :], in1=rs[:], op=ALU.mult), reads=[bs_, brs], writes=[bd[c]])

                def sq_part(srcs, bsrcs):
                    out = []
                    for s_ in srcs:
                        sq, bsq = sq8.next()
                        C.op("act", lambda e: e.activation(out=sq[:], in_=s_, func=AF.Square), reads=bsrcs, writes=[bsq])
                        out.append((sq, bsq))
                    return out

                def stat_part(sqs, nfeat):
                    pst, bps = pn.next()
                    n = len(sqs)
                    for i, (sq, bsq) in enumerate(sqs):
                        C.mm(pst[:], [(onesb[:], sq[:])], reads=[bsq, bones], writes=[bps], start=(i == 0), stop=(i == n - 1))
                    ln_, bln = lnr.next()
                    C.op("act", lambda e: e.activation(out=ln_[:], in_=pst[:], func=AF.Ln, bias=epst[:, 0:1], scale=1.0 / nfeat), reads=[bps, beps], writes=[bln])
                    o, bo = rsr.next()
                    C.op("act", lambda e: e.activation(out=o[:], in_=ln_[:], func=AF.Exp, scale=-0.5), reads=[bln], writes=[bo])
                    return o, bo

                def A_sq(g):
                    ts_ = slice(g * 512, (g + 1) * 512)
                    mx, bmx = mxr.next()
                    C.dma("sp", mx[:], mixv[:, :, ts_], reads=[bMIX], writes=[bmx])
                    h, bh = hr.next()
                    C.dma("sp", h[:], hview[:, :, ts_], reads=[bhsrc], writes=[bh])
                    pp, bpp = ppr.next()
                    C.dma("sp", pp[:], pT[l].rearrange("(c p) t -> p c t", p=128)[:, :, ts_], writes=[bpp])
                    sqs = sq_part([mx[:, c, :] for c in range(8)], [bmx])
                    pb, bpb = pbr.next()
                    C.op("pool", lambda e: e.tensor_copy(out=pb[:], in_=pp[:]), reads=[bpp], writes=[bpb])
                    G[g] = dict(h=h, bh=bh, mx=mx, bmx=bmx, sqs=sqs, pb=pb, bpb=bpb, ts=ts_)

                def A_stat(g):
                    st = G[g]
                    mx, bmx = st["mx"], st["bmx"]
                    mn, bmn = nbr.next()
                    for (c0, c1) in ((0, 3), (3, 5), (5, 8)):
                        rs, brs = stat_part(st["sqs"][c0:c1], 128 * (c1 - c0))
                        scale_bf(mn, bmn, mx, bmx, rs, brs, range(c0, c1))
                    st["mn"], st["bmn"] = mn, bmn

                def A_mm(g, dcs):
                    st = G[g]
                    h, bh, mn, bmn = st["h"], st["bh"], st["mn"], st["bmn"]
                    for dc in dcs:
                        ps, bps = pz.next()
                        C.mm(ps[:], [(wout[:, kc, dc * 128:(dc + 1) * 128], mn[:, kc, :]) for kc in range(8)], reads=[bwout, bmn], writes=[bps])
                        C.op("dve", lambda e: e.tensor_tensor(out=h[:, dc, :], in0=ps[:], in1=h[:, dc, :], op=ALU.add), reads=[bps, bh], writes=[bh])

                def B_sq(g):
                    st = G[g]
                    st["sqs"] = sq_part([st["h"][:, c, :] for c in range(8)], [st["bh"]])

                def B_stat(g):
                    st = G[g]
                    h, bh = st["h"], st["bh"]
                    rs, brs = stat_part(st["sqs"], 1024)
                    m_, bm = nbr.next()
                    scale_bf(m_, bm, h, bh, rs, brs, range(8))
                    st["m"], st["bm"] = m_, bm

                def B_mm(g):
                    st = G[g]
                    m_, bm = st["m"], st["bm"]
                    pend = None

                    def tail(k, tg, btg, tv, btv):
                        sg, bsg = sgr.next()
                        C.op("act", lambda e: e.activation(out=sg[:], in_=tg[:], func=AF.Silu), reads=[btg], writes=[bsg])
                        C.op("pool", lambda e: e.tensor_tensor(out=actt[:, k, :], in0=sg[:], in1=tv[:], op=ALU.mult), reads=[bsg, btv], writes=[bact[k]])
                        if k == NFC - 7:
                            C.dma("pool", RELAY[0:1, 0:16], actt[0:1, k, 0:16], reads=bact[:k + 1], writes=[brelay])

                    for k in range(NFC):
                        wu, bwu = wupr.next()
                        C.dma("sp", wu[:], wupv[:, :, k, :], reads=wbufs(l, OFF_WUP, OFF_WUP + 8 * 5632), writes=[bwu])
                        res = []
                        ri = xr2.i
                        xt_, bxs_ = xr2.next()
                        bxh = xhb2[ri]
                        bh2 = [bhalo[2 * k], bhalo[2 * k + 1]]
                        C.op("pool", lambda e: e.tensor_copy(out=xt_[:, :, 0:2], in_=halo[:, 2 * k:2 * k + 2, :]), reads=bh2, writes=[bxh])
                        ev = []
                        for kind in range(2):
                            ps, bps = pz.next()
                            C.mm(ps[:], [(wu[:, kc, kind * 128:(kind + 1) * 128], m_[:, kc, :]) for kc in range(8)], reads=[bwu, bm], writes=[bps])
                            idx = 2 * k + kind
                            fw = lambda j, idx=idx: svt[:, l, SV_FW + j * 44 + idx:SV_FW + j * 44 + idx + 1]
                            fb = svt[:, l, SV_FB + idx:SV_FB + idx + 1]
                            t_, bt = tr_[kind].next()
                            bx = bxs_[kind]
                            C.op("act", lambda e: e.copy(out=xt_[:, kind, 2:514], in_=ps[:]), reads=[bps], writes=[bx])
                            C.op("act", lambda e: e.activation(out=t_[:], in_=ps[:], func=AF.Identity, bias=fb, scale=fw(2)), reads=[bps, bsvt], writes=[bt])
                            ev.append((kind, fw, t_, bt, bx))
                        C.op("pool", lambda e: e.tensor_copy(out=halo[:, 2 * k:2 * k + 2, :], in_=xt_[:, :, 512:514]), reads=bxs_, writes=bh2)
                        for (kind, fw, t_, bt, bx) in ev:
                            C.op("dve", lambda e: e.scalar_tensor_tensor(out=t_[:], in0=xt_[:, kind, 1:513], scalar=fw(1), in1=t_[:], op0=ALU.mult, op1=ALU.add),
                                 reads=[bx, bxh, bsvt, bt], writes=[bt])
                            C.op("dve", lambda e: e.scalar_tensor_tensor(out=t_[:], in0=xt_[:, kind, 0:512], scalar=fw(0), in1=t_[:], op0=ALU.mult, op1=ALU.add),
                                 reads=[bx, bxh, bsvt, bt], writes=[bt])
                            res.append((t_, bt))
                        if pend is not None:
                            tail(*pend)
                        pend = (k, res[0][0], res[0][1], res[1][0], res[1][1])
                        if k in (6, 10, 14):
                            wd_load(g, (k - 6) // 4)
                        if k == 4 and deferred:
                            deferred.pop(0)()
                    tail(*pend)

                def wd_load(g, dc):
                    wd, bwd = wdnr.next()
                    C.dma("sp", wd[:], wdnv[:, dc, :, :], reads=wbufs(l, OFF_WDN, OFF_WDN + 8 * 2816), writes=[bwd])
                    G[g].setdefault("wd", {})[dc] = (wd, bwd)

                def C_mm(g, hook=None):
                    st = G[g]
                    h, bh = st["h"], st["bh"]
                    KS = NFC - 6

                    def getwd(dc):
                        if dc not in st.get("wd", {}):
                            wd_load(g, dc)
                        return st["wd"].pop(dc)

                    def part1(dc, ps, bps, wd, bwd):
                        C.mm(ps[:], [(wd[:, k, :], actt[:, k, :]) for k in range(KS)], reads=[bwd, brelay], writes=[bps], start=True, stop=False)

                    def part2(dc, ps, bps, wd, bwd):
                        C.mm(ps[:], [(wd[:, k, :], actt[:, k, :]) for k in range(KS, NFC)], reads=[bwd] + bact[KS:], writes=[bps], start=False, stop=True)
                        C.op("dve", lambda e: e.tensor_tensor(out=h[:, dc, :], in0=ps[:], in1=h[:, dc, :], op=ALU.add), reads=[bps, bh], writes=[bh])

                    first = []
                    for dc in (0, 1):
                        wd, bwd = getwd(dc)
                        ps, bps = pn.next()
                        part1(dc, ps, bps, wd, bwd)
                        first.append((dc, ps, bps, wd, bwd))
                    for i_, args in enumerate(first):
                        part2(*args)
                        wd_load(g, 3 + i_)
                    for dc in range(2, 8):
                        wd, bwd = getwd(dc)
                        ps, bps = pz.next()
                        part1(dc, ps, bps, wd, bwd)
                        part2(dc, ps, bps, wd, bwd)
                        if dc + 3 < 8:
                            wd_load(g, dc + 3)
                        if dc == 3 and hook is not None:
                            hook()

                def D_sq(g):
                    st = G[g]
                    st["sqs"] = sq_part([st["h"][:, c, :] for c in range(8)], [st["bh"]])

                def D_stat(g):
                    st = G[g]
                    h, bh = st["h"], st["bh"]
                    rs, brs = stat_part(st["sqs"], 1024)
                    hn, bhn = nbr.next()
                    scale_bf(hn, bhn, h, bh, rs, brs, range(8))
                    st["hn"], st["bhn"] = hn, bhn

                def D_mm(g, hook=None):
                    st = G[g]
                    h, bh, hn, bhn, pb, bpb = st["h"], st["bh"], st["hn"], st["bhn"], st["pb"], st["bpb"]
                    for dc in range(8):
                        ps, bps = pz.next()
                        C.mm(ps[:], [(wpg[:, kc, dc * 128:(dc + 1) * 128], hn[:, kc, :]) for kc in range(8)], reads=[bwpg, bhn], writes=[bps])
                        ps2, bps2 = pz.next()
                        C.mm(ps2[:], [(wple[:, kc, dc * 128:(dc + 1) * 128], pb[:, kc, :]) for kc in range(2)], reads=[bwple, bpb], writes=[bps2])
                        sg, bsg = sgr.next()
                        C.op("act", lambda e: e.activation(out=sg[:], in_=ps[:], func=AF.Sigmoid), reads=[bps], writes=[bsg])
                        C.op("dve", lambda e: e.tensor_tensor(out=sg[:], in0=ps2[:], in1=sg[:], op=ALU.mult), reads=[bps2, bsg], writes=[bsg])
                        C.op("pool", lambda e: e.tensor_tensor(out=h[:, dc, :], in0=sg[:], in1=h[:, dc, :], op=ALU.add), reads=[bsg, bh], writes=[bh])
                        if dc == 3 and hook is not None:
                            hook()
                    ts_fin = st["ts"]

                    def fin():
                        if last:
                            rs, brs = rms_bc([h[:, c, :] for c in range(8)], [bh], 1024, sq8, pn, lnr, rsr)
                            for c in range(8):
                                C.op("dve", lambda e: e.scalar_tensor_tensor(out=h[:, c, :], in0=h[:, c, :], scalar=svt[:, l, SV_FN + c:SV_FN + c + 1], in1=rs[:],
                                                                             op0=ALU.mult, op1=ALU.mult), reads=[bh, brs, bsvt], writes=[bh])
                        C.dma("pool", dview[:, :, ts_fin], h[:], reads=[bh], writes=[bdst])

                    deferred.append(fin)
                    del G[g]

                sq8 = Ring(C, es, "sq8p", [128, 512], BF16, 8)
                deferred = []
                A_sq(0)
                A_stat(0)
                A_mm(0, range(8))
                B_sq(0)
                B_stat(0)
                B_mm(0)
                for g in range(NG5):
                    nx = g + 1 < NG5
                    if nx:
                        A_sq(g + 1)
                    C_mm(g, hook=(lambda: A_stat(g + 1)) if nx else None)
                    D_sq(g)
                    if nx:
                        A_mm(g + 1, range(0, 4))
                    D_stat(g)
                    if nx:
                        A_mm(g + 1, range(4, 8))
                        B_sq(g + 1)
                    D_mm(g, hook=(lambda: B_stat(g + 1)) if nx else None)
                    if nx:
                        B_mm(g + 1)
                while deferred:
                    deferred.pop(0)()
                C.fence()
            es_w.close()
            hsrc, bhsrc = HT, bHT
        C.fence()
    return nc


_NC_CACHE = {}


def _host_inputs(inputs, S, L):
    w = {k: np.asarray(v) for k, v in inputs.items()}
    imgs, gs, svs = [], [], []
    for i in range(L):
        a, b, c = _weight_image(i, w)
        imgs.append(a)
        gs.append(b)
        svs.append(c)
    wimg = np.stack(imgs)
    gimg = np.stack(gs)
    svimg = np.stack(svs)
    cimg = _consts()
    B = w["x"].shape[0]
    maps = []
    for b in range(B):
        maps.append({
            "xT": np.ascontiguousarray(w["x"][b].T),
            "pT": np.ascontiguousarray(w["p"][:L, b].transpose(0, 2, 1)),
            "pos": np.ascontiguousarray(w["positions"][b].reshape(1, S).astype(np.int32)),
            "wimg": wimg, "gimg": gimg, "svimg": svimg, "cimg": cimg,
        })
    return maps


def kernel(**inputs):
    x = np.asarray(inputs["x"])
    B, S, _ = x.shape
    L = np.asarray(inputs["w_in"]).shape[0]
    key = (S, L)
    if key not in _NC_CACHE:
        _NC_CACHE[key] = build(S, L)
    nc = _NC_CACHE[key]
    maps = _host_inputs(inputs, S, L)
    res = run_bass_kernel_spmd(nc, maps, core_ids=list(range(B)))
    out = np.stack([np.ascontiguousarray(r["outT"].T) for r in res.results])
    return out.astype(np.float32)
```

```python
import math
import contextlib
import numpy as np
import concourse.bass as bass
import concourse.mybir as mybir
from concourse.bass_utils import run_bass_kernel_spmd

F32 = mybir.dt.float32
BF16 = mybir.dt.bfloat16
I32 = mybir.dt.int32
AF = mybir.ActivationFunctionType
ALU = mybir.AluOpType

D = 1024
NH = 6
DFF = 2816
NFC = 22
SEQ = 4096
DEPTH = 2
EPS = 1e-6

NWIN = 2504
C_ZQ, C_ZKV, C_ZRA, C_ZRB, C_ZB, C_ZC, C_ZH, C_FQ, C_FK, C_FV, C_FF = 0, 256, 384, 480, 576, 832, 1088, 1344, 1728, 2112, 2496
OFF_WIN = 0
OFF_WUQ = OFF_WIN + 8 * NWIN
OFF_WUKV = OFF_WUQ + 2 * 1152
OFF_WOUT = OFF_WUKV + 768
OFF_WUP = OFF_WOUT + 8 * 1024
OFF_WDN = OFF_WUP + 8 * 5632
OFF_WPG = OFF_WDN + 8 * 2816
OFF_WPLE = OFF_WPG + 8 * 1024
TOT = OFF_WPLE + 2 * 1024
GA, GQ, GKV, GO, GF, GP, NG = 0, 8, 10, 11, 19, 27, 35
SV_CW, SV_FW, SV_FB, SV_BF, SV_FN, NSV = 0, 6, 138, 182, 183, 191
CS_ID, CS_TRI, CS_IF, NCST = 0, 128, 256, 258

NDS = 24


def _weight_image(i, w):
    img = np.zeros((128, TOT), np.float32)
    w_in = w["w_in"][i]
    for kc in range(8):
        r = w_in[kc * 128:(kc + 1) * 128]
        b = OFF_WIN + kc * NWIN
        img[:, b + C_ZQ:b + C_ZQ + 384] = r[:, 0:384]
        img[:, b + C_ZRA + 64:b + C_ZRA + 96] = r[:, 384:416]
        img[:, b + C_ZRB + 64:b + C_ZRB + 80] = r[:, 400:416]
        img[:, b + C_ZRB + 80:b + C_ZRB + 96] = r[:, 384:400]
        img[:, b + C_ZB:b + C_ZB + 1926] = r[:, 416:2342]
    w_uq = w["w_uq"][i]
    for kc in range(2):
        rh = w_uq[kc * 128:(kc + 1) * 128].reshape(128, NH, 96)
        b = OFF_WUQ + kc * 1152
        img[:, b:b + 384] = rh[:, :, 0:64].reshape(128, 384)
        img[:, b + 384:b + 576] = rh[:, :, 64:96].reshape(128, 192)
        rb = np.concatenate([rh[:, :, 80:96], rh[:, :, 64:80]], axis=2)
        img[:, b + 576:b + 768] = rb.reshape(128, 192)
    w_ukv = w["w_ukv"][i].reshape(128, NH, 2, 64)
    img[:, OFF_WUKV:OFF_WUKV + 384] = w_ukv[:, :, 0, :].reshape(128, 384)
    img[:, OFF_WUKV + 384:OFF_WUKV + 768] = w_ukv[:, :, 1, :].reshape(128, 384)
    img[:, OFF_WOUT:OFF_WOUT + 8192] = w["w_out"][i].reshape(8, 128, 1024).transpose(1, 0, 2).reshape(128, 8192)
    w_up = w["w_up"][i].reshape(8, 128, 2, NFC, 128)
    img[:, OFF_WUP:OFF_WUP + 8 * 5632] = w_up.transpose(1, 0, 3, 2, 4).reshape(128, 8 * 5632)
    w_dn = w["w_down"][i].reshape(NFC, 128, 8, 128)
    img[:, OFF_WDN:OFF_WDN + 8 * 2816] = w_dn.transpose(1, 2, 0, 3).reshape(128, 8 * 2816)
    img[:, OFF_WPG:OFF_WPG + 8192] = w["w_ple_gate"][i].reshape(8, 128, 1024).transpose(1, 0, 2).reshape(128, 8192)
    img[:, OFF_WPLE:OFF_WPLE + 2048] = w["w_ple"][i].reshape(2, 128, 1024).transpose(1, 0, 2).reshape(128, 2048)
    g = np.zeros((128, NG), np.float32)
    g[:, GA:GA + 8] = w["attn_norm"][i].reshape(8, 128).T
    g[:, GQ:GQ + 2] = w["q_norm"][i].reshape(2, 128).T
    g[:, GKV] = w["kv_norm"][i]
    on = np.concatenate([w["mla_out_norm"][i], w["conv_out_norm"][i], w["fox_out_norm"][i]])
    g[:, GO:GO + 8] = on.reshape(8, 128).T
    g[:, GF:GF + 8] = w["ffn_norm"][i].reshape(8, 128).T
    g[:, GP:GP + 8] = w["ple_norm"][i].reshape(8, 128).T
    sv = np.zeros((128, NSV), np.float32)
    sv[:, SV_CW:SV_CW + 6] = w["conv_w"][i].reshape(3, 2, 128).transpose(2, 0, 1).reshape(128, 6)
    fw = w["ffn_conv_w"][i].reshape(3, 2, NFC, 128)
    sv[:, SV_FW:SV_FW + 132] = fw.transpose(3, 0, 2, 1).reshape(128, 132)
    fb = w["ffn_conv_b"][i].reshape(2, NFC, 128)
    sv[:, SV_FB:SV_FB + 44] = fb.transpose(2, 1, 0).reshape(128, 44)
    sv[0:NH, SV_BF] = w["b_forget"][i]
    sv[:, SV_FN:SV_FN + 8] = w["final_norm"].reshape(8, 128).T
    return img, g, sv


def _consts():
    c = np.zeros((128, NCST), np.float32)
    c[:, CS_ID:CS_ID + 128] = np.eye(128, dtype=np.float32)
    c[:, CS_TRI:CS_TRI + 128] = np.triu(np.ones((128, 128), np.float32))
    inv = (10000.0 ** (-np.arange(0, 32, 2, dtype=np.float32) / 32)).astype(np.float32)
    for p in range(64, 96):
        c[p, CS_IF] = inv[(p - 64) % 16]
    return c


class Buf:
    __slots__ = ("w", "r")

    def __init__(self):
        self.w = None
        self.r = {}


class Eng:
    def __init__(self, name, eng, sem, sid):
        self.name, self.e, self.sem, self.sid = name, eng, sem, sid
        self.cnt = 0
        self.waited = {}


class KC:
    def __init__(self, nc, es):
        self.nc, self.es = nc, es
        self.semh = {}
        self.E = {}
        sid = 0
        for n, e in (("pe", nc.tensor), ("act", nc.scalar), ("dve", nc.vector), ("pool", nc.gpsimd), ("sp", nc.sync)):
            h = es.enter_context(nc.semaphore("s_" + n))
            self.semh[sid] = h
            self.E[n] = Eng(n, e, h, sid)
            sid += 1
        self.dsem = {"sp": [], "pool": []}
        for q in ("sp", "pool"):
            for i in range(NDS // 2):
                h = es.enter_context(nc.semaphore("d%s%d" % (q, i)))
                self.semh[sid] = h
                self.dsem[q].append([sid, 0])
                sid += 1
        self.dnext = {"sp": 0, "pool": 0}
        self.uid = 0

    @staticmethod
    def _flat(bs):
        out = []
        for b in bs:
            if isinstance(b, (list, tuple)):
                out.extend(KC._flat(b))
            else:
                out.append(b)
        return out

    def _deps(self, reads, writes):
        reads, writes = self._flat(reads), self._flat(writes)
        d = {}
        for b in reads:
            if b.w is not None:
                s, v = b.w
                if d.get(s, 0) < v:
                    d[s] = v
        for b in writes:
            if b.w is not None:
                s, v = b.w
                if d.get(s, 0) < v:
                    d[s] = v
            for s, v in b.r.items():
                if d.get(s, 0) < v:
                    d[s] = v
        return d

    def _wait(self, E, deps, skip_self=False):
        for s, v in deps.items():
            if skip_self and s == E.sid:
                continue
            if E.waited.get(s, 0) < v:
                E.e.wait_ge(self.semh[s], v)
                E.waited[s] = v

    def _mark(self, ev, reads, writes):
        reads, writes = self._flat(reads), self._flat(writes)
        s, v = ev
        for b in reads:
            if b.r.get(s, 0) < v:
                b.r[s] = v
        for b in writes:
            b.w = ev
            b.r = {}

    def op(self, en, fn, reads=(), writes=()):
        E = self.E[en]
        self._wait(E, self._deps(reads, writes), skip_self=(en == "pe"))
        ins = fn(E.e)
        E.cnt += 1
        ins.then_inc(E.sem, 1)
        self._mark((E.sid, E.cnt), reads, writes)

    def mm(self, out, pairs, reads=(), writes=(), start=True, stop=True):
        E = self.E["pe"]
        self._wait(E, self._deps(reads, writes), skip_self=True)
        n = len(pairs)
        ins = None
        for i, (l, r) in enumerate(pairs):
            ins = E.e.matmul(out, lhsT=l, rhs=r, start=(start and i == 0), stop=(stop and i == n - 1))
        E.cnt += 1
        ins.then_inc(E.sem, 1)
        self._mark((E.sid, E.cnt), reads, writes)

    def dma(self, qn, out, in_, reads=(), writes=()):
        E = self.E[qn]
        deps = self._deps(reads, writes)
        slot = self.dsem[qn][self.dnext[qn]]
        self.dnext[qn] = (self.dnext[qn] + 1) % (NDS // 2)
        sid, cnt = slot
        if cnt > 0 and deps.get(sid, 0) < 16 * cnt:
            deps[sid] = 16 * cnt
        self._wait(E, deps)
        ins = E.e.dma_start(out=out, in_=in_)
        slot[1] = cnt + 1
        ins.then_inc(self.semh[sid], 16)
        self._mark((sid, 16 * (cnt + 1)), reads, writes)

    def fence(self, engines=("pe", "act", "dve", "pool", "sp")):
        cur = {}
        for E in self.E.values():
            if E.cnt > 0:
                cur[E.sid] = E.cnt
        for q in self.dsem.values():
            for sid, cnt in q:
                if cnt > 0:
                    cur[sid] = 16 * cnt
        for en in engines:
            self._wait(self.E[en], dict(cur))


class Ring:
    def __init__(self, C, es, name, shape, dt, n, psum=False):
        self.t = []
        for i in range(n):
            C.uid += 1
            nm = "%s_%d_%d" % (name, i, C.uid)
            if psum:
                self.t.append(es.enter_context(C.nc.psum_tensor(nm, list(shape), dt)))
            else:
                self.t.append(es.enter_context(C.nc.sbuf_tensor(nm, list(shape), dt)))
        self.b = [Buf() for _ in range(n)]
        self.i = 0
        self.n = n

    def next(self):
        k = self.i
        self.i = (self.i + 1) % self.n
        return self.t[k], self.b[k]


def tile(C, es, name, shape, dt):
    C.uid += 1
    return es.enter_context(C.nc.sbuf_tensor("%s_%d" % (name, C.uid), list(shape), dt)), Buf()


def build(S=SEQ, L=DEPTH, dbg=False):
    assert S % 512 == 0
    NG5 = S // 512
    NT = S // 128
    nc = bass.Bass("TRN2", target_bir_lowering=False)
    okind = "ExternalOutput" if dbg else "Internal"
    xT = nc.dram_tensor("xT", [D, S], F32, kind="ExternalInput").ap()
    pT = nc.dram_tensor("pT", [L, 256, S], F32, kind="ExternalInput").ap()
    pos = nc.dram_tensor("pos", [1, S], I32, kind="ExternalInput").ap()
    wimg = nc.dram_tensor("wimg", [L, 128, TOT], F32, kind="ExternalInput").ap()
    gimg = nc.dram_tensor("gimg", [L, 128, NG], F32, kind="ExternalInput").ap()
    svimg = nc.dram_tensor("svimg", [L, 128, NSV], F32, kind="ExternalInput").ap()
    cimg = nc.dram_tensor("cimg", [128, NCST], F32, kind="ExternalInput").ap()
    outT = nc.dram_tensor("outT", [D, S], F32, kind="ExternalOutput").ap()
    WB = nc.dram_tensor("WB", [L, 128, TOT], BF16).ap()
    ROPE = nc.dram_tensor("ROPE", [2, 32, S], F32, kind=okind).ap()
    HT = nc.dram_tensor("HT", [D, S], F32, kind=okind).ap()
    MIX = nc.dram_tensor("MIX", [D, S], F32, kind=okind).ap()
    QM = nc.dram_tensor("QM", [NH, 96, S], BF16, kind=okind).ap()
    KM = nc.dram_tensor("KM", [NH, 64, S], BF16, kind=okind).ap()
    KR = nc.dram_tensor("KR", [32, S], BF16, kind=okind).ap()
    VM = nc.dram_tensor("VM", [128, NT, NH, 128], BF16, kind=okind).ap()
    QF = nc.dram_tensor("QF", [NH, 64, S], BF16, kind=okind).ap()
    KF = nc.dram_tensor("KF", [NH, 64, S], BF16, kind=okind).ap()
    VF = nc.dram_tensor("VF", [128, NT, NH, 128], BF16, kind=okind).ap()
    CA = nc.dram_tensor("CA", [NH, 6, S], BF16, kind=okind).ap()
    bWB = [Buf() for _ in range(L)]
    RELAY = nc.dram_tensor("RELAY", [1, 16], BF16).ap()
    bROPE, bHT, bMIX = Buf(), Buf(), Buf()
    bQM, bKM, bKR, bVM, bQF, bKF, bVF, bCA, bNCM = (Buf() for _ in range(9))
    bQMr = [Buf() for _ in range(NH)]

    top = contextlib.ExitStack()
    with top:
        C = KC(nc, top)
        cst, bcst = tile(C, top, "cst", [128, NCST], F32)
        gt, bgt = tile(C, top, "gt", [128, L, NG], F32)
        svt, bsvt = tile(C, top, "svt", [128, L, NSV], F32)
        nbf, bnbf = tile(C, top, "nbf", [128, L], F32)
        onesb, bones = tile(C, top, "onesb", [128, 128], BF16)
        trib, btri = tile(C, top, "trib", [128, 128], BF16)
        id6, bid6 = cst, bcst
        C.dma("sp", cst[:], cimg[:, :], writes=[bcst])
        for l in range(L):
            C.dma("sp", gt[:, l, :], gimg[l], writes=[bgt])
            C.dma("sp", svt[:, l, :], svimg[l], writes=[bsvt])
        C.op("dve", lambda e: e.memset(onesb[:], 1.0), writes=[bones])
        C.op("dve", lambda e: e.tensor_copy(out=trib[:], in_=cst[:, CS_TRI:CS_TRI + 128]), reads=[bcst], writes=[btri])
        identb, bidb = tile(C, top, "identb", [128, 128], BF16)
        negm, bnegm = tile(C, top, "negm", [128, 128], BF16)
        C.op("dve", lambda e: e.tensor_copy(out=identb[:], in_=cst[:, CS_ID:CS_ID + 128]), reads=[bcst], writes=[bidb])
        C.op("dve", lambda e: e.tensor_scalar(out=negm[:], in0=cst[:, CS_TRI:CS_TRI + 128], scalar1=-1.0, scalar2=29952.0, op0=ALU.add, op1=ALU.mult),
             reads=[bcst], writes=[bnegm])
        for l in range(L):
            C.op("dve", lambda e: e.tensor_scalar(out=nbf[:, l:l + 1], in0=svt[:, l, SV_BF:SV_BF + 1], scalar1=-1.0, scalar2=None, op0=ALU.mult),
                 reads=[bsvt], writes=[bnbf])

        epst, beps = tile(C, top, "epst", [128, 2], F32)
        C.op("dve", lambda e: e.memset(epst[:, 0:1], EPS), writes=[beps])
        C.op("dve", lambda e: e.memset(epst[:, 1:2], 1.0), writes=[beps])
        es0 = contextlib.ExitStack()
        win0, bwin0 = tile(C, es0, "win0", [128, 8, NWIN], BF16)
        wuq0, bwuq0 = tile(C, es0, "wuq0", [128, 2, 1152], BF16)
        wukv0, bwukv0 = tile(C, es0, "wukv0", [128, 768], BF16)
        pieces = []
        for kc in range(8):
            pieces += [(OFF_WIN + kc * NWIN, 1252, GA + kc), (OFF_WIN + kc * NWIN + 1252, 1252, GA + kc)]
        for kc in range(2):
            pieces.append((OFF_WUQ + kc * 1152, 1152, GQ + kc))
        pieces.append((OFF_WUKV, 768, GKV))
        NP1 = len(pieces)
        for kc in range(8):
            pieces.append((OFF_WOUT + kc * 1024, 1024, GO + kc))
        for kc in range(8):
            for q in range(4):
                pieces.append((OFF_WUP + kc * 5632 + q * 1408, 1408, GF + kc))
        for q in range(11):
            pieces.append((OFF_WDN + q * 2048, 2048, None))
        for kc in range(8):
            pieces.append((OFF_WPG + kc * 1024, 1024, GP + kc))
        pieces.append((OFF_WPLE, 2048, None))
        negs = []
        for kc in range(8):
            negs.append(OFF_WIN + kc * NWIN + C_ZRB + 64)
        for kc in range(2):
            for h in range(NH):
                negs.append(OFF_WUQ + kc * 1152 + 576 + h * 32)
        bW = [[Buf() for _ in pieces] for _ in range(L)]

        def wbufs(l, lo, hi):
            return [bW[l][i] for i, (c0, n, gc) in enumerate(pieces) if c0 < hi and c0 + n > lo]

        def prep_gen(items, sf, sb, engines):
            pend = None
            for cnt, (l, pi) in enumerate(items):
                c0, n, gc = pieces[pi]
                f, bf_ = sf.next()
                b, bb = sb.next()
                C.dma("sp", f[:, 0:n], wimg[l, :, c0:c0 + n], writes=[bf_])
                en = engines[cnt % len(engines)]
                if gc is None:
                    if en == "act":
                        C.op("act", lambda e: e.copy(out=b[:, 0:n], in_=f[:, 0:n]), reads=[bf_], writes=[bb])
                    else:
                        C.op(en, lambda e: e.tensor_copy(out=b[:, 0:n], in_=f[:, 0:n]), reads=[bf_], writes=[bb])
                else:
                    if en == "act":
                        C.op("act", lambda e: e.mul(out=b[:, 0:n], in_=f[:, 0:n], mul=gt[:, l, gc:gc + 1]), reads=[bf_, bgt], writes=[bb])
                    else:
                        C.op(en, lambda e: e.tensor_scalar(out=b[:, 0:n], in0=f[:, 0:n], scalar1=gt[:, l, gc:gc + 1], scalar2=None, op0=ALU.mult),
                             reads=[bf_, bgt], writes=[bb])
                for ng in negs:
                    if c0 <= ng < c0 + n:
                        o = ng - c0
                        C.op("dve", lambda e: e.tensor_scalar(out=b[:, o:o + 16], in0=b[:, o:o + 16], scalar1=-1.0, scalar2=None, op0=ALU.mult),
                             reads=[bb], writes=[bb])
                if pend is not None:
                    C.dma("pool", *pend[0], reads=pend[1], writes=pend[2])
                pend = ((WB[l, :, c0:c0 + n], b[:, 0:n]), [bb], [bW[l][pi]])
                yield
            if pend is not None:
                C.dma("pool", *pend[0], reads=pend[1], writes=pend[2])
            yield

        with contextlib.ExitStack() as es:
            sf = Ring(C, es, "sf", [128, 2048], F32, 3)
            sb = Ring(C, es, "sb", [128, 2048], BF16, 3)
            for pi in range(NP1):
                c0, n, gc = pieces[pi]
                f, bf_ = sf.next()
                C.dma("sp", f[:, 0:n], wimg[0, :, c0:c0 + n], writes=[bf_])
                if c0 < OFF_WUQ:
                    kc_, off_ = divmod(c0 - OFF_WIN, NWIN)
                    dst, bd = win0[:, kc_, off_:off_ + n], bwin0
                elif c0 < OFF_WUKV:
                    dst, bd = wuq0[:, (c0 - OFF_WUQ) // 1152, 0:n], bwuq0
                else:
                    dst, bd = wukv0[:, 0:n], bwukv0
                C.op("act", lambda e: e.mul(out=dst, in_=f[:, 0:n], mul=gt[:, 0, gc:gc + 1]), reads=[bf_, bgt], writes=[bd])
                for ng in negs:
                    if c0 <= ng < c0 + n:
                        o = ng - c0
                        C.op("dve", lambda e: e.tensor_scalar(out=dst[:, o:o + 16], in0=dst[:, o:o + 16], scalar1=-1.0, scalar2=None, op0=ALU.mult),
                             reads=[bd], writes=[bd])
            pi_, bpi = tile(C, es, "pi", [128, S], I32)
            pf, bpf = tile(C, es, "pf", [128, S], F32)
            ang, bang = tile(C, es, "ang", [128, S], F32)
            kq, bkq = tile(C, es, "kq", [128, S], F32)
            ki, bki = tile(C, es, "ki", [128, S], I32)
            rr, brr = tile(C, es, "rr", [128, S], F32)
            sn, bsn = tile(C, es, "sn", [128, S], F32)
            R = slice(64, 96)
            C1 = 6.28125
            C2 = 2 * math.pi - C1
            C.dma("sp", pi_[R, :], pos[0:1, :].to_broadcast([32, S]), writes=[bpi])
            C.op("dve", lambda e: e.tensor_copy(out=pf[R, :], in_=pi_[R, :]), reads=[bpi], writes=[bpf])
            C.op("dve", lambda e: e.tensor_scalar(out=ang[R, :], in0=pf[R, :], scalar1=cst[R, CS_IF:CS_IF + 1], scalar2=None, op0=ALU.mult),
                 reads=[bpf, bcst], writes=[bang])
            for which in (0, 1):
                if which == 0:
                    C.op("dve", lambda e: e.tensor_scalar(out=pf[R, :], in0=ang[R, :], scalar1=math.pi / 2, scalar2=None, op0=ALU.add),
                         reads=[bang], writes=[bpf])
                    src, bsrc = pf, bpf
                else:
                    src, bsrc = ang, bang
                C.op("dve", lambda e: e.tensor_scalar(out=kq[R, :], in0=src[R, :], scalar1=1.0 / (2 * math.pi), scalar2=None, op0=ALU.mult),
                     reads=[bsrc], writes=[bkq])
                C.op("dve", lambda e: e.tensor_copy(out=ki[R, :], in_=kq[R, :]), reads=[bkq], writes=[bki])
                C.op("dve", lambda e: e.tensor_copy(out=kq[R, :], in_=ki[R, :]), reads=[bki], writes=[bkq])
                C.op("dve", lambda e: e.scalar_tensor_tensor(out=rr[R, :], in0=kq[R, :], scalar=-C1, in1=src[R, :], op0=ALU.mult, op1=ALU.add),
                     reads=[bkq, bsrc], writes=[brr])
                C.op("dve", lambda e: e.scalar_tensor_tensor(out=rr[R, :], in0=kq[R, :], scalar=-C2, in1=rr[R, :], op0=ALU.mult, op1=ALU.add),
                     reads=[bkq, brr], writes=[brr])
                C.op("dve", lambda e: e.tensor_scalar(out=rr[R, :], in0=rr[R, :], scalar1=-3.1415925, scalar2=3.1415925, op0=ALU.max, op1=ALU.min),
                     reads=[brr], writes=[brr])
                C.op("act", lambda e: e.activation(out=sn[R, :], in_=rr[R, :], func=AF.Sin), reads=[brr], writes=[bsn])
                C.dma("sp", ROPE[which], sn[R, :], reads=[bsn], writes=[bROPE])
            C.fence()

        def rms_bc(srcs, bsrcs, nfeat, sqr, pn, lnr, outr):
            pst, bps = pn.next()
            n = len(srcs)
            sqs = []
            for i, s_ in enumerate(srcs):
                sq, bsq = sqr.next()
                C.op("act", lambda e: e.activation(out=sq[:], in_=s_, func=AF.Square), reads=bsrcs, writes=[bsq])
                C.mm(pst[:], [(onesb[:], sq[:])], reads=[bsq, bones], writes=[bps], start=(i == 0), stop=(i == n - 1))
            ln_, bln = lnr.next()
            C.op("act", lambda e: e.activation(out=ln_[:], in_=pst[:], func=AF.Ln, bias=epst[:, 0:1], scale=1.0 / nfeat), reads=[bps, beps], writes=[bln])
            o, bo = outr.next()
            C.op("act", lambda e: e.activation(out=o[:], in_=ln_[:], func=AF.Exp, scale=-0.5), reads=[bln], writes=[bo])
            return o, bo


        hsrc, bhsrc = xT, Buf()
        for l in range(L):
            wb = WB[l]
            with contextlib.ExitStack() as es:
                if l == 0:
                    win, bwin, wuq, bwuq, wukv, bwukv = win0, bwin0, wuq0, bwuq0, wukv0, bwukv0
                    bwinA = bwinB = [bwin0]
                else:
                    win, bwin = tile(C, es, "win", [128, 8, NWIN], BF16)
                    wuq, bwuq = tile(C, es, "wuq", [128, 2, 1152], BF16)
                    wukv, bwukv = tile(C, es, "wukv", [128, 768], BF16)
                    bwinA = [Buf() for _ in range(8)]
                    bwinB = [Buf() for _ in range(8)]
                    for kc in range(8):
                        o = OFF_WIN + kc * NWIN
                        C.dma("sp", win[:, kc, 0:C_ZB], wb[:, o:o + C_ZB], reads=wbufs(l, o, o + C_ZB), writes=[bwinA[kc]])
                    for kc in range(8):
                        o = OFF_WIN + kc * NWIN
                        C.dma("sp", win[:, kc, C_ZB:NWIN], wb[:, o + C_ZB:o + NWIN], reads=wbufs(l, o + C_ZB, o + NWIN), writes=[bwinB[kc]])
                    C.dma("sp", wuq[:], wb[:, OFF_WUQ:OFF_WUQ + 2304].rearrange("p (c n) -> p c n", c=2), reads=wbufs(l, OFF_WUQ, OFF_WUQ + 2304), writes=[bwuq])
                    C.dma("sp", wukv[:], wb[:, OFF_WUKV:OFF_WUKV + 768], reads=wbufs(l, OFF_WUKV, OFF_WUKV + 768), writes=[bwukv])
                hr = Ring(C, es, "h32", [128, 8, 512], F32, 2)
                ar = Ring(C, es, "abf", [128, 8, 512], BF16, 1)
                ar.b = [[Buf() for _ in range(8)] for _ in range(ar.n)]
                sqr = Ring(C, es, "sq", [128, 512], BF16, 3)
                lnr = Ring(C, es, "ln", [128, 512], F32, 2)
                rsr = Ring(C, es, "rstd", [128, 512], F32, 3)
                pn = Ring(C, es, "pn", [128, 512], F32, 2, psum=True)
                pz = Ring(C, es, "pz", [128, 512], F32, 6, psum=True)
                zq_sb, bzq = tile(C, es, "zq_sb", [128, 2, 512], F32)
                zqn, bzqn = tile(C, es, "zqn", [128, 2, 512], BF16)
                zkv_sb, bzkv = tile(C, es, "zkv_sb", [128, 512], F32)
                zkvn, bzkvn = tile(C, es, "zkvn", [128, 512], BF16)
                roper = Ring(C, es, "rope", [128, 2, 512], F32, 2)
                qmr = Ring(C, es, "qms", [64, NH, 512], BF16, 1)
                ror = Ring(C, es, "ro", [128, 512], BF16, 2)
                kmr = Ring(C, es, "kms", [64, NH, 512], BF16, 1)
                krr = Ring(C, es, "krs", [32, 512], BF16, 1)
                vmr = Ring(C, es, "vms", [128, 4, NH, 128], BF16, 1)
                qfr = Ring(C, es, "qfs", [64, NH, 512], BF16, 1)
                kfr = Ring(C, es, "kfs", [64, NH, 512], BF16, 1)
                vfr = Ring(C, es, "vfs", [128, 4, NH, 128], BF16, 1)
                tar = Ring(C, es, "ta", [128, 512], F32, 2)
                tbr = Ring(C, es, "tb", [128, 512], F32, 2)
                zh_sb, bzh = tile(C, es, "zh_sb", [128, 2, 512], F32)
                xs, bxs = tile(C, es, "xs", [128, 2, 514], F32)
                t1, bt1 = tile(C, es, "t1", [128, 2, 512], F32)
                ocr = Ring(C, es, "ocs", [128, 2, 512], F32, 1)
                ex, bex = tile(C, es, "ex", [8, 512], F32)
                ncum, bncum = tile(C, es, "ncum", [8, 512], F32)
                carry, bcarry = tile(C, es, "carry", [8, 1], F32)
                one6, bone6 = tile(C, es, "one6", [8, 512], F32)
                c8, bc8 = ex, bex
                r1, br1 = tile(C, es, "r1", [8, 512], F32)
                car = Ring(C, es, "cas", [8, 6, 512], BF16, 1)
                for rg in (vmr, vfr):
                    for k in range(rg.n):
                        C.op("pool", lambda e: e.memset(rg.t[k][:, :, :, 64:128], 1.0), writes=[rg.b[k]])
                C.op("dve", lambda e: e.memset(xs[:, :, 0:2], 0.0), writes=[bxs])
                C.op("dve", lambda e: e.memset(carry[:], 0.0), writes=[bcarry])
                C.op("dve", lambda e: e.memset(one6[:], 1.0), writes=[bone6])
                hview = hsrc.rearrange("(c p) t -> p c t", p=128)

                def evac(i, out, in_, reads, writes):
                    if i % 4 != 3:
                        C.op("act", lambda e: e.copy(out=out, in_=in_), reads=reads, writes=writes)
                    else:
                        C.op("dve", lambda e: e.tensor_copy(out=out, in_=in_), reads=reads, writes=writes)

                GS = {}

                def norm_load(g):
                    ts_ = slice(g * 512, (g + 1) * 512)
                    h, bh = hr.next()
                    C.dma("sp", h[:], hview[:, :, ts_], reads=[bhsrc], writes=[bh])
                    rope, brope = roper.next()
                    for rep in range(4):
                        C.dma("sp", rope[32 * rep:32 * rep + 32, :, :], ROPE[:, :, ts_].rearrange("w p t -> p w t"), reads=[bROPE], writes=[brope])
                    GS[g] = dict(h=h, bh=bh, rope=rope, brope=brope, ts=ts_)

                def norm_a(g):
                    st = GS[g]
                    h, bh = st["h"], st["bh"]
                    sqs = []
                    for c in range(8):
                        sq, bsq = (sq8 if c < 6 else sqr).next()
                        C.op("act", lambda e: e.activation(out=sq[:], in_=h[:, c, :], func=AF.Square), reads=[bh], writes=[bsq])
                        sqs.append((sq, bsq))
                    st["sqs"] = sqs

                def norm_b(g):
                    st = GS[g]
                    h, bh = st["h"], st["bh"]
                    pst, bps = pn.next()
                    for c, (sq, bsq) in enumerate(st["sqs"]):
                        C.mm(pst[:], [(onesb[:], sq[:])], reads=[bsq, bones], writes=[bps], start=(c == 0), stop=(c == 7))
                    ln_, bln = lnr.next()
                    C.op("act", lambda e: e.activation(out=ln_[:], in_=pst[:], func=AF.Ln, bias=epst[:, 0:1], scale=1.0 / 1024), reads=[bps, beps], writes=[bln])
                    rs, brs = rsr.next()
                    C.op("act", lambda e: e.activation(out=rs[:], in_=ln_[:], func=AF.Exp, scale=-0.5), reads=[bln], writes=[brs])
                    a, ba = ar.next()
                    st["a"], st["ba"], st["rs"], st["brs"] = a, ba, rs, brs

                def norm_c(g, chunks):
                    st = GS[g]
                    h, bh, a, ba, rs, brs = st["h"], st["bh"], st["a"], st["ba"], st["rs"], st["brs"]
                    for c in chunks:
                        C.op("dve", lambda e: e.tensor_tensor(out=a[:, c, :], in0=h[:, c, :], in1=rs[:], op=ALU.mult), reads=[bh, brs], writes=[ba[c]])

                def main(g):
                    st = GS[g]
                    a, ba, rope, brope, ts_ = st["a"], st["ba"], st["rope"], st["brope"], st["ts"]

                    def zmm(col, m):
                        ps, bps = pz.next()
                        C.mm(ps[0:m, :], [(win[:, kc, col:col + m], a[:, kc, :]) for kc in range(8)], reads=[bwinA if col + m <= C_ZB else bwinB, ba], writes=[bps])
                        return ps, bps

                    if g + 1 < NG5:
                        norm_load(g + 1)
                    for m in range(2):
                        ps, bps = zmm(C_ZQ + 128 * m, 128)
                        evac(0, zq_sb[:, m, :], ps[:], [bps], [bzq])
                    ps, bps = zmm(C_ZKV, 128)
                    evac(0, zkv_sb[:], ps[:], [bps], [bzkv])
                    sqq = []
                    for m in range(2):
                        sq, bsq = sqr.next()
                        C.op("act", lambda e: e.activation(out=sq[:], in_=zq_sb[:, m, :], func=AF.Square), reads=[bzq], writes=[bsq])
                        sqq.append((sq, bsq))
                    sqk, bsqk = sqr.next()
                    C.op("act", lambda e: e.activation(out=sqk[:], in_=zkv_sb[:], func=AF.Square), reads=[bzkv], writes=[bsqk])
                    oc, boc = ocr.next()
                    for c in range(2):
                        pzh, bpzh = zmm(C_ZH + 128 * c, 128)
                        C.op("act", lambda e: e.copy(out=zh_sb[:, c, :], in_=pzh[:]), reads=[bpzh], writes=[bzh])
                        pzc, bpzc = zmm(C_ZC + 128 * c, 128)
                        C.op("dve", lambda e: e.tensor_tensor(out=xs[:, c, 2:514], in0=pzc[:], in1=zh_sb[:, c, :], op=ALU.mult), reads=[bpzc, bzh, bxs], writes=[bxs])
                        cw = lambda j: svt[:, l, SV_CW + 2 * j + c:SV_CW + 2 * j + c + 1]
                        C.op("dve", lambda e: e.tensor_scalar(out=t1[:, c, :], in0=xs[:, c, 2:514], scalar1=cw(2), scalar2=None, op0=ALU.mult), reads=[bxs, bsvt], writes=[bt1])
                        C.op("dve", lambda e: e.scalar_tensor_tensor(out=t1[:, c, :], in0=xs[:, c, 1:513], scalar=cw(1), in1=t1[:, c, :], op0=ALU.mult, op1=ALU.add),
                             reads=[bxs, bsvt, bt1], writes=[bt1])
                        C.op("dve", lambda e: e.scalar_tensor_tensor(out=t1[:, c, :], in0=xs[:, c, 0:512], scalar=cw(0), in1=t1[:, c, :], op0=ALU.mult, op1=ALU.add),
                             reads=[bxs, bsvt, bt1], writes=[bt1])
                        pzb, bpzb = zmm(C_ZB + 128 * c, 128)
                        C.op("dve", lambda e: e.tensor_tensor(out=oc[:, c, :], in0=pzb[:], in1=t1[:, c, :], op=ALU.mult), reads=[bpzb, bt1], writes=[boc])
                        C.op("pool", lambda e: e.tensor_copy(out=xs[:, c, 0:2], in_=xs[:, c, 512:514]), reads=[bxs], writes=[bxs])
                    C.dma("pool", MIX.rearrange("(c p) t -> p c t", p=128)[:, 3:5, ts_], oc[:], reads=[boc], writes=[bMIX])
                    pst, bpsq = pn.next()
                    for m, (sq, bsq) in enumerate(sqq):
                        C.mm(pst[:], [(onesb[:], sq[:])], reads=[bsq, bones], writes=[bpsq], start=(m == 0), stop=(m == 1))
                    ln_, bln = lnr.next()
                    C.op("act", lambda e: e.activation(out=ln_[:], in_=pst[:], func=AF.Ln, bias=epst[:, 0:1], scale=1.0 / 256), reads=[bpsq, beps], writes=[bln])
                    rq, brq = rsr.next()
                    C.op("act", lambda e: e.activation(out=rq[:], in_=ln_[:], func=AF.Exp, scale=-0.5), reads=[bln], writes=[brq])
                    pst2, bpsk = pn.next()
                    C.mm(pst2[:], [(onesb[:], sqk[:])], reads=[bsqk, bones], writes=[bpsk])
                    ln2, bln2 = lnr.next()
                    C.op("act", lambda e: e.activation(out=ln2[:], in_=pst2[:], func=AF.Ln, bias=epst[:, 0:1], scale=1.0 / 128), reads=[bpsk, beps], writes=[bln2])
                    rk, brk = rsr.next()
                    C.op("act", lambda e: e.activation(out=rk[:], in_=ln2[:], func=AF.Exp, scale=-0.5), reads=[bln2], writes=[brk])
                    for m in range(2):
                        C.op("dve", lambda e: e.tensor_tensor(out=zqn[:, m, :], in0=zq_sb[:, m, :], in1=rq[:], op=ALU.mult), reads=[bzq, brq], writes=[bzqn])
                    C.op("dve", lambda e: e.tensor_tensor(out=zkvn[:], in0=zkv_sb[:], in1=rk[:], op=ALU.mult), reads=[bzkv, brk], writes=[bzkvn])
                    pa, bpa = zmm(C_ZRA + 64, 32)
                    pb, bpb = zmm(C_ZRB + 64, 32)
                    ta, bta = tar.next()
                    tb, btb = tbr.next()
                    krs, bkrs = krr.next()
                    C.op("dve", lambda e: e.tensor_tensor(out=ta[0:32, :], in0=pa[0:32, :], in1=rope[0:32, 0, :], op=ALU.mult), reads=[bpa, brope], writes=[bta])
                    C.op("dve", lambda e: e.tensor_tensor(out=tb[0:32, :], in0=pb[0:32, :], in1=rope[0:32, 1, :], op=ALU.mult), reads=[bpb, brope], writes=[btb])
                    C.op("pool", lambda e: e.tensor_tensor(out=krs[0:32, :], in0=ta[0:32, :], in1=tb[0:32, :], op=ALU.add), reads=[bta, btb], writes=[bkrs])
                    C.dma("pool", KR[:, ts_], krs[0:32, :], reads=[bkrs], writes=[bKR])
                    pff, bpff = zmm(C_FF, 6)
                    C.op("act", lambda e: e.activation(out=ex[0:6, :], in_=pff[0:6, :], func=AF.Exp, bias=nbf[0:6, l:l + 1], scale=-1.0), reads=[bpff, bnbf], writes=[bex])
                    C.op("act", lambda e: e.activation(out=ex[0:6, :], in_=ex[0:6, :], func=AF.Ln, bias=epst[0:6, 1:2], scale=1.0), reads=[bex, beps], writes=[bex])
                    C.op("dve", lambda e: e.tensor_tensor_scan(out=ncum[0:6, :], data0=one6[0:6, :], data1=ex[0:6, :], initial=carry[0:6, 0:1], op0=ALU.mult, op1=ALU.add),
                         reads=[bone6, bex, bcarry], writes=[bncum])
                    C.op("dve", lambda e: e.tensor_copy(out=carry[0:6, 0:1], in_=ncum[0:6, 511:512]), reads=[bncum], writes=[bcarry])
                    cas, bcas = car.next()
                    C.op("dve", lambda e: e.tensor_scalar(out=c8[0:6, :], in0=ncum[0:6, :], scalar1=-8.0, scalar2=None, op0=ALU.mult), reads=[bncum], writes=[bc8])
                    C.op("dve", lambda e: e.tensor_copy(out=cas[0:6, 0, :], in_=c8[0:6, :]), reads=[bc8], writes=[bcas])
                    C.op("dve", lambda e: e.tensor_tensor(out=r1[0:6, :], in0=c8[0:6, :], in1=cas[0:6, 0, :], op=ALU.subtract), reads=[bc8, bcas], writes=[br1])
                    C.op("dve", lambda e: e.tensor_copy(out=cas[0:6, 1, :], in_=r1[0:6, :]), reads=[br1], writes=[bcas])
                    C.op("dve", lambda e: e.tensor_tensor(out=c8[0:6, :], in0=r1[0:6, :], in1=cas[0:6, 1, :], op=ALU.subtract), reads=[br1, bcas, bc8], writes=[bc8])
                    C.op("dve", lambda e: e.tensor_copy(out=cas[0:6, 2, :], in_=c8[0:6, :]), reads=[bc8], writes=[bcas])
                    C.op("dve", lambda e: e.tensor_scalar(out=cas[0:6, 3:6, :], in0=cas[0:6, 0:3, :], scalar1=-1.0, scalar2=None, op0=ALU.mult), reads=[bcas], writes=[bcas])
                    C.dma("pool", CA[:, :, ts_], cas[0:6, :, :], reads=[bcas], writes=[bCA])
                    qfs, bqfs = qfr.next()
                    kfs, bkfs = kfr.next()
                    for hp in range(3):
                        pq, bpq = zmm(C_FQ + 128 * hp, 128)
                        evac(0, qfs[:, 2 * hp, :], pq[0:64, :], [bpq], [bqfs])
                        evac(1, qfs[:, 2 * hp + 1, :], pq[64:128, :], [bpq], [bqfs])
                        pk, bpk = zmm(C_FK + 128 * hp, 128)
                        evac(2, kfs[:, 2 * hp, :], pk[0:64, :], [bpk], [bkfs])
                        evac(3, kfs[:, 2 * hp + 1, :], pk[64:128, :], [bpk], [bkfs])
                    C.dma("pool", QF.rearrange("h p t -> p h t")[:, :, ts_], qfs[:], reads=[bqfs], writes=[bQF])
                    C.dma("pool", KF.rearrange("h p t -> p h t")[:, :, ts_], kfs[:], reads=[bkfs], writes=[bKF])
                    vfs, bvfs = vfr.next()
                    for tt in range(4):
                        pv, bpv = pz.next()
                        C.mm(pv[:, 0:384], [(a[:, kc, tt * 128:(tt + 1) * 128], win[:, kc, C_FV:C_FV + 384]) for kc in range(8)], reads=[bwinB, ba], writes=[bpv])
                        evac(tt, vfs[:, tt, :, 0:64], pv[:, 0:384].rearrange("p (h d) -> p h d", h=NH), [bpv], [bvfs])
                    g4 = st["ts"].start // 128
                    C.dma("pool", VF[:, g4:g4 + 4, :, :], vfs[:], reads=[bvfs], writes=[bVF])
                    if g + 1 < NG5:
                        norm_a(g + 1)
                    qms, bqms = qmr.next()
                    nxg = g + 1 < NG5
                    for hp in range(3):
                        ps, bps = pz.next()
                        C.mm(ps[:, :], [(wuq[:, kc, 128 * hp:128 * hp + 128], zqn[:, kc, :]) for kc in range(2)], reads=[bwuq, bzqn], writes=[bps])
                        C.op("act", lambda e: e.copy(out=qms[:, 2 * hp, :], in_=ps[0:64, :]), reads=[bps], writes=[bqms])
                        C.op("act", lambda e: e.copy(out=qms[:, 2 * hp + 1, :], in_=ps[64:128, :]), reads=[bps], writes=[bqms])
                        if nxg:
                            if hp == 0:
                                norm_b(g + 1)
                            else:
                                norm_c(g + 1, (2 * hp - 2, 2 * hp - 1))
                    C.dma("pool", QM.rearrange("h p t -> p h t")[0:64, :, ts_], qms[:], reads=[bqms], writes=[bQM])
                    for gi, (h0, nh_) in enumerate(((0, 4), (4, 2))):
                        m = 32 * nh_
                        pa, bpa = pz.next()
                        C.mm(pa[0:m, :], [(wuq[:, kc, 384 + 32 * h0:384 + 32 * h0 + m], zqn[:, kc, :]) for kc in range(2)], reads=[bwuq, bzqn], writes=[bpa])
                        pb, bpb = pz.next()
                        C.mm(pb[0:m, :], [(wuq[:, kc, 576 + 32 * h0:576 + 32 * h0 + m], zqn[:, kc, :]) for kc in range(2)], reads=[bwuq, bzqn], writes=[bpb])
                        ta, bta = tar.next()
                        tb, btb = tbr.next()
                        ro, bro = ror.next()
                        C.op("dve", lambda e: e.tensor_tensor(out=ta[0:m, :], in0=pa[0:m, :], in1=rope[0:m, 0, :], op=ALU.mult), reads=[bpa, brope], writes=[bta])
                        C.op("dve", lambda e: e.tensor_tensor(out=tb[0:m, :], in0=pb[0:m, :], in1=rope[0:m, 1, :], op=ALU.mult), reads=[bpb, brope], writes=[btb])
                        C.op("pool", lambda e: e.tensor_tensor(out=ro[0:m, :], in0=ta[0:m, :], in1=tb[0:m, :], op=ALU.add), reads=[bta, btb], writes=[bro])
                        for hq in range(nh_):
                            C.dma("pool", QM[h0 + hq, 64:96, ts_], ro[32 * hq:32 * hq + 32, :], reads=[bro], writes=[bQMr[h0 + hq]])
                        if nxg:
                            norm_c(g + 1, (4 + 2 * gi, 5 + 2 * gi))
                    kms, bkms = kmr.next()
                    for hp in range(3):
                        pk, bpk = pz.next()
                        C.mm(pk[:, :], [(wukv[:, hp * 128:(hp + 1) * 128], zkvn[:])], reads=[bwukv, bzkvn], writes=[bpk])
                        evac(0, kms[:, 2 * hp, :], pk[0:64, :], [bpk], [bkms])
                        evac(1, kms[:, 2 * hp + 1, :], pk[64:128, :], [bpk], [bkms])
                    C.dma("pool", KM.rearrange("h p t -> p h t")[:, :, ts_], kms[:], reads=[bkms], writes=[bKM])
                    vms, bvms = vmr.next()
                    for tt in range(4):
                        pv, bpv = pz.next()
                        C.mm(pv[:, 0:384], [(zkvn[:, tt * 128:(tt + 1) * 128], wukv[:, 384:768])], reads=[bwukv, bzkvn], writes=[bpv])
                        evac(tt, vms[:, tt, :, 0:64], pv[:, 0:384].rearrange("p (h d) -> p h d", h=NH), [bpv], [bvms])
                    C.dma("pool", VM[:, g4:g4 + 4, :, :], vms[:], reads=[bvms], writes=[bVM])
                    del GS[g]

                sq8 = Ring(C, es, "sq8", [128, 512], BF16, 6)
                norm_load(0)
                norm_a(0)
                norm_b(0)
                norm_c(0, range(8))
                for g in range(NG5):
                    main(g)
                C.fence()

            if l == 0:
                es0.close()
            es_w = contextlib.ExitStack()
            wout, bwout = tile(C, es_w, "wout", [128, 8, 1024], BF16)
            wpg, bwpg = tile(C, es_w, "wpg", [128, 8, 1024], BF16)
            wple, bwple = tile(C, es_w, "wple", [128, 2, 1024], BF16)
            with contextlib.ExitStack() as es:
                kr_ = {"m": Ring(C, es, "Km", [96, S], BF16, 2), "f": Ring(C, es, "Kf", [96, S], BF16, 2)}
                qr_ = {"m": Ring(C, es, "Qm", [96, S], BF16, 2), "f": Ring(C, es, "Qf", [96, S], BF16, 2)}
                vr_ = Ring(C, es, "V", [128, NT, 128], BF16, 2)
                NCH = 4
                CW = S // NCH
                for rg in (kr_["m"], kr_["f"], qr_["m"], qr_["f"], vr_):
                    rg.b = [[Buf() for _ in range(NCH)] for _ in range(rg.n)]
                pss = Ring(C, es, "pss", [128, 512], F32, 5, psum=True)
                pso = Ring(C, es, "pso", [128, 512], F32, 3, psum=True)
                pr = Ring(C, es, "pexp", [128, 512], BF16, 6)
                rcr = Ring(C, es, "rc", [64, 512], F32, 2)
                sf2 = Ring(C, es, "sf2", [128, 2048], F32, 3)
                sb2 = Ring(C, es, "sb2", [128, 2048], BF16, 3)
                items = [(l, pi) for pi in range(NP1, len(pieces))]
                if l + 1 < L:
                    items += [(l + 1, pi) for pi in range(NP1)]
                pgen = prep_gen(items, sf2, sb2, ("dve",))
                osr = Ring(C, es, "ost", [64, 512], F32, 3)
                for k in range(2):
                    C.op("pool", lambda e: e.memset(kr_["f"].t[k][64:70, :], 1.0), writes=[kr_["f"].b[k]])
                    C.op("pool", lambda e: e.memset(qr_["f"].t[k][64:70, :], 1.0), writes=[qr_["f"].b[k]])
                basem, bbm = tile(C, es, "basem", [128, 512], F32)
                basef, bbf = tile(C, es, "basef", [128, 512], F32)
                C.op("dve", lambda e: e.memset(basem[:], math.exp(96 ** -0.5)), writes=[bbm])
                C.op("dve", lambda e: e.memset(basef[:], math.exp(0.125)), writes=[bbf])
                ssr = Ring(C, es, "ssb", [128, 512], F32, 2)
                cnt_off = [0]
                blocks = []
                for mixer in ("m", "f"):
                    for hh in range(NH):
                        for j in range(NG5):
                            for i in range(4 * j + 4):
                                blocks.append((mixer, hh, j, i))
                heads = {}
                head_order = [(mx_, hh) for mx_ in ("m", "f") for hh in range(NH)]
                accs = {}
                inflight = {}

                def load_head(mixer, hh):
                    Kt, bK = kr_[mixer].next()
                    Qt, bQ = qr_[mixer].next()
                    Vt, bV = vr_.next()
                    nct, bnct = None, None
                    if mixer == "m" and not heads:
                        for c in range(NCH):
                            cs = slice(c * CW, (c + 1) * CW)
                            tsl = slice(c * (NT // NCH), (c + 1) * (NT // NCH))
                            C.dma("sp", Kt[0:64, cs], KM[hh][:, cs], reads=[bKM], writes=[bK[c]])
                            C.dma("sp", Kt[64:96, cs], KR[:, cs], reads=[bKR], writes=[bK[c]])
                            C.dma("sp", Qt[0:96, cs], QM[hh][:, cs], reads=[bQM, bQMr[hh]], writes=[bQ[c]])
                            C.dma("sp", Vt[:, tsl, :], VM[:, tsl, hh, :], reads=[bVM], writes=[bV[c]])
                    elif mixer == "m":
                        C.dma("sp", Kt[0:64, :], KM[hh], reads=[bKM], writes=[bK])
                        C.dma("sp", Kt[64:96, :], KR[:, :], reads=[bKR], writes=[bK])
                        C.dma("sp", Qt[0:96, :], QM[hh], reads=[bQM, bQMr[hh]], writes=[bQ])
                        C.dma("sp", Vt[:], VM[:, :, hh, :], reads=[bVM], writes=[bV])
                    else:
                        C.dma("sp", Kt[0:64, :], KF[hh], reads=[bKF], writes=[bK])
                        C.dma("sp", Qt[0:64, :], QF[hh], reads=[bQF], writes=[bQ])
                        C.dma("sp", Qt[64:67, :], CA[hh, 0:3, :], reads=[bCA], writes=[bQ])
                        C.dma("sp", Kt[67:70, :], CA[hh, 3:6, :], reads=[bCA], writes=[bK])
                        C.dma("sp", Vt[:], VF[:, :, hh, :], reads=[bVF], writes=[bV])
                    heads[(mixer, hh)] = (Kt, bK, Qt, bQ, Vt, bV, nct, bnct)

                def qk(n):
                    mixer, hh, j, i = blocks[n]
                    if (mixer, hh) not in heads:
                        load_head(mixer, hh)
                    Kt, bK, Qt, bQ, Vt, bV, nct, bnct = heads[(mixer, hh)]
                    Kd = 96 if mixer == "m" else 70
                    r = i - 4 * j
                    c0 = 128 * r if r > 0 else 0
                    sps, bs = pss.next()
                    if r >= 0:
                        C.mm(sps[:, c0:512], [(Kt[0:Kd, i * 128:(i + 1) * 128], Qt[0:Kd, j * 512 + c0:(j + 1) * 512])], reads=[bK[(i * 128) // CW], bQ[(j * 512) // CW:((j + 1) * 512 - 1) // CW + 1]], writes=[bs],
                             start=True, stop=False)
                        C.mm(sps[:, c0:c0 + 128], [(identb[:], negm[:])], reads=[bidb, bnegm], writes=[bs], start=False, stop=True)
                    else:
                        C.mm(sps[:, c0:512], [(Kt[0:Kd, i * 128:(i + 1) * 128], Qt[0:Kd, j * 512 + c0:(j + 1) * 512])], reads=[bK[(i * 128) // CW], bQ[(j * 512) // CW:((j + 1) * 512 - 1) // CW + 1]], writes=[bs])
                    inflight[n] = (sps, bs, c0, r)

                def rest(n):
                    mixer, hh, j, i = blocks[n]
                    Kt, bK, Qt, bQ, Vt, bV, nct, bnct = heads[(mixer, hh)]
                    sps, bs, c0, r = inflight.pop(n)
                    sc = 96 ** -0.5 if mixer == "m" else 0.125
                    row0 = 0 if mixer == "m" else 640
                    ntile = 4 * j + 4
                    if i == 0:
                        accs[(mixer, hh, j)] = pso.next()
                    oacc, bo = accs[(mixer, hh, j)]
                    pt, bp = pr.next()
                    use_pool = False
                    if r < 0:
                        cnt_off[0] += 1
                        use_pool = False
                    if use_pool:
                        ssb, bss = ssr.next()
                        bt_, bbt = (basem, bbm) if mixer == "m" else (basef, bbf)
                        C.op("dve", lambda e: e.tensor_copy(out=ssb[:], in_=sps[:]), reads=[bs], writes=[bss])
                        C.op("pool", lambda e: e.tensor_tensor(out=pt[:], in0=bt_[:], in1=ssb[:], op=ALU.pow), reads=[bss, bbt], writes=[bp])
                    else:
                        C.op("act", lambda e: e.activation(out=pt[:, c0:512], in_=sps[:, c0:512], func=AF.Exp, scale=sc), reads=[bs], writes=[bp])
                    C.mm(oacc[:, c0:512], [(Vt[:, i, :], pt[:, c0:512])], reads=[bV[(i * 128) // CW], bp], writes=[bo], start=(i == 0), stop=(i == ntile - 1))
                    if j == 0 and i == 0:
                        hi_ = head_order.index((mixer, hh))
                        if hi_ + 1 < len(head_order) and head_order[hi_ + 1] not in heads:
                            load_head(*head_order[hi_ + 1])
                    if i == ntile - 1:
                        del accs[(mixer, hh, j)]
                        rc, brc = rcr.next()
                        ost, bost = osr.next()
                        C.op("dve", lambda e: e.reciprocal(out=rc[0:64, :], in_=oacc[64:128, :]), reads=[bo], writes=[brc])
                        C.op("dve", lambda e: e.tensor_tensor(out=ost[0:64, :], in0=oacc[0:64, :], in1=rc[0:64, :], op=ALU.mult), reads=[bo, brc], writes=[bost])
                        C.dma("pool", MIX[row0 + hh * 64:row0 + (hh + 1) * 64, j * 512:(j + 1) * 512], ost[0:64, :], reads=[bost], writes=[bMIX])

                LA = 4
                NB = len(blocks)
                pstep = max(1, (NB - 8) // (len(items) + 1))
                owed = [0]
                for n in range(NB + LA):
                    if n < NB:
                        qk(n)
                    if n >= LA:
                        rest(n - LA)
                    if n % pstep == pstep - 1:
                        owed[0] += 1
                    if owed[0] and blocks[min(n, NB - 1)][2] >= 3:
                        owed[0] -= 1
                        next(pgen, None)
                for _ in pgen:
                    pass
                C.dma("sp", wout[:], wb[:, OFF_WOUT:OFF_WOUT + 8192].rearrange("p (c n) -> p c n", c=8), reads=wbufs(l, OFF_WOUT, OFF_WOUT + 8192), writes=[bwout])
                C.dma("sp", wpg[:], wb[:, OFF_WPG:OFF_WPG + 8192].rearrange("p (c n) -> p c n", c=8), reads=wbufs(l, OFF_WPG, OFF_WPG + 8192), writes=[bwpg])
                C.dma("sp", wple[:], wb[:, OFF_WPLE:OFF_WPLE + 2048].rearrange("p (c n) -> p c n", c=2), reads=wbufs(l, OFF_WPLE, OFF_WPLE + 2048), writes=[bwple])
                C.fence()

            with contextlib.ExitStack() as es:
                mxr = Ring(C, es, "mx", [128, 8, 512], F32, 1)
                hr = Ring(C, es, "h", [128, 8, 512], F32, 2)
                nbr = Ring(C, es, "nb", [128, 8, 512], BF16, 2)
                nbr.b = [[Buf() for _ in range(8)] for _ in range(nbr.n)]
                actt, _ba = tile(C, es, "actt", [128, NFC, 512], BF16)
                bact = [Buf() for _ in range(NFC)]
                brelay = Buf()
                wupr = Ring(C, es, "wup", [128, 8, 256], BF16, 3)
                wdnr = Ring(C, es, "wdn", [128, NFC, 128], BF16, 3)
                ppr = Ring(C, es, "pp", [128, 2, 512], F32, 1)
                pbr = Ring(C, es, "pb", [128, 2, 512], BF16, 2)
                sqr = None
                lnr = Ring(C, es, "ln", [128, 512], F32, 1)
                rsr = Ring(C, es, "rstd", [128, 512], F32, 3)
                xr2 = Ring(C, es, "xgv", [128, 2, 514], F32, 2)
                xr2.b = [[Buf(), Buf()] for _ in range(xr2.n)]
                xhb2 = [Buf() for _ in range(xr2.n)]
                tr_ = {0: Ring(C, es, "tg", [128, 512], F32, 3), 1: Ring(C, es, "tv", [128, 512], F32, 3)}
                sgr = Ring(C, es, "sg", [128, 512], F32, 2)
                halo, _bh = tile(C, es, "halo", [128, 2 * NFC, 2], F32)
                bhalo = [Buf() for _ in range(2 * NFC)]
                pn = Ring(C, es, "pn", [128, 512], F32, 2, psum=True)
                pz = Ring(C, es, "pz", [128, 512], F32, 6, psum=True)
                C.op("dve", lambda e: e.memset(halo[:], 0.0), writes=bhalo)
                mixv = MIX.rearrange("(c p) t -> p c t", p=128)
                hview = hsrc.rearrange("(c p) t -> p c t", p=128)
                last = (l == L - 1)
                dst, bdst = (outT, Buf()) if last else (HT, bHT)
                dview = dst.rearrange("(c p) t -> p c t", p=128)
                wupv = wb[:, OFF_WUP:OFF_WUP + 8 * 5632].rearrange("p (c k n) -> p c k n", c=8, k=NFC)
                wdnv = wb[:, OFF_WDN:OFF_WDN + 8 * 2816].rearrange("p (d k n) -> p d k n", d=8, k=NFC)
                G = {}

                def scale_bf(dstt, bd, src, bs_, rs, brs, chunks):
                    for n_, c in enumerate(chunks):
                        en = "dve" if n_ % 3 != 2 else "pool"
                        C.op(en, lambda e: e.tensor_tensor(out=dstt[:, c, :], in0=src[:, c, :], in1=rs[:], op=ALU.mult), reads=[bs_, brs], writes=[bd[c]])

                def sq_part(srcs, bsrcs):
                    out = []
                    for s_ in srcs:
                        sq, bsq = sq8.next()
                        C.op("act", lambda e: e.activation(out=sq[:], in_=s_, func=AF.Square), reads=bsrcs, writes=[bsq])
                        out.append((sq, bsq))
                    return out

                def stat_part(sqs, nfeat):
                    pst, bps = pn.next()
                    n = len(sqs)
                    for i, (sq, bsq) in enumerate(sqs):
                        C.mm(pst[:], [(onesb[:], sq[:])], reads=[bsq, bones], writes=[bps], start=(i == 0), stop=(i == n - 1))
                    ln_, bln = lnr.next()
                    C.op("act", lambda e: e.activation(out=ln_[:], in_=pst[:], func=AF.Ln, bias=epst[:, 0:1], scale=1.0 / nfeat), reads=[bps, beps], writes=[bln])
                    o, bo = rsr.next()
                    C.op("act", lambda e: e.activation(out=o[:], in_=ln_[:], func=AF.Exp, scale=-0.5), reads=[bln], writes=[bo])
                    return o, bo

                def A_sq(g):
                    ts_ = slice(g * 512, (g + 1) * 512)
                    mx, bmx = mxr.next()
                    C.dma("sp", mx[:], mixv[:, :, ts_], reads=[bMIX], writes=[bmx])
                    h, bh = hr.next()
                    C.dma("sp", h[:], hview[:, :, ts_], reads=[bhsrc], writes=[bh])
                    pp, bpp = ppr.next()
                    C.dma("sp", pp[:], pT[l].rearrange("(c p) t -> p c t", p=128)[:, :, ts_], writes=[bpp])
                    sqs = sq_part([mx[:, c, :] for c in range(8)], [bmx])
                    pb, bpb = pbr.next()
                    C.op("pool", lambda e: e.tensor_copy(out=pb[:], in_=pp[:]), reads=[bpp], writes=[bpb])
                    G[g] = dict(h=h, bh=bh, mx=mx, bmx=bmx, sqs=sqs, pb=pb, bpb=bpb, ts=ts_)

                def A_stat(g):
                    st = G[g]
                    mx, bmx = st["mx"], st["bmx"]
                    mn, bmn = nbr.next()
                    for (c0, c1) in ((0, 3), (3, 5), (5, 8)):
                        rs, brs = stat_part(st["sqs"][c0:c1], 128 * (c1 - c0))
                        scale_bf(mn, bmn, mx, bmx, rs, brs, range(c0, c1))
                    st["mn"], st["bmn"] = mn, bmn

                def A_mm(g, dcs):
                    st = G[g]
                    h, bh, mn, bmn = st["h"], st["bh"], st["mn"], st["bmn"]
                    for dc in dcs:
                        ps, bps = pz.next()
                        C.mm(ps[:], [(wout[:, kc, dc * 128:(dc + 1) * 128], mn[:, kc, :]) for kc in range(8)], reads=[bwout, bmn], writes=[bps])
                        C.op("dve", lambda e: e.tensor_tensor(out=h[:, dc, :], in0=ps[:], in1=h[:, dc, :], op=ALU.add), reads=[bps, bh], writes=[bh])

                def B_sq(g):
                    st = G[g]
                    st["sqs"] = sq_part([st["h"][:, c, :] for c in range(8)], [st["bh"]])

                def B_stat(g):
                    st = G[g]
                    h, bh = st["h"], st["bh"]
                    rs, brs = stat_part(st["sqs"], 1024)
                    m_, bm = nbr.next()
                    scale_bf(m_, bm, h, bh, rs, brs, range(8))
                    st["m"], st["bm"] = m_, bm

                def B_mm(g):
                    st = G[g]
                    m_, bm = st["m"], st["bm"]
                    pend = None

                    def tail(k, tg, btg, tv, btv):
                        sg, bsg = sgr.next()
                        C.op("act", lambda e: e.activation(out=sg[:], in_=tg[:], func=AF.Silu), reads=[btg], writes=[bsg])
                        C.op("pool", lambda e: e.tensor_tensor(out=actt[:, k, :], in0=sg[:], in1=tv[:], op=ALU.mult), reads=[bsg, btv], writes=[bact[k]])
                        if k == NFC - 7:
                            C.dma("pool", RELAY[0:1, 0:16], actt[0:1, k, 0:16], reads=bact[:k + 1], writes=[brelay])

                    for k in range(NFC):
                        wu, bwu = wupr.next()
                        C.dma("sp", wu[:], wupv[:, :, k, :], reads=wbufs(l, OFF_WUP, OFF_WUP + 8 * 5632), writes=[bwu])
                        res = []
                        ri = xr2.i
                        xt_, bxs_ = xr2.next()
                        bxh = xhb2[ri]
                        bh2 = [bhalo[2 * k], bhalo[2 * k + 1]]
                        C.op("pool", lambda e: e.tensor_copy(out=xt_[:, :, 0:2], in_=halo[:, 2 * k:2 * k + 2, :]), reads=bh2, writes=[bxh])
                        ev = []
                        for kind in range(2):
                            ps, bps = pz.next()
                            C.mm(ps[:], [(wu[:, kc, kind * 128:(kind + 1) * 128], m_[:, kc, :]) for kc in range(8)], reads=[bwu, bm], writes=[bps])
                            idx = 2 * k + kind
                            fw = lambda j, idx=idx: svt[:, l, SV_FW + j * 44 + idx:SV_FW + j * 44 + idx + 1]
                            fb = svt[:, l, SV_FB + idx:SV_FB + idx + 1]
                            t_, bt = tr_[kind].next()
                            bx = bxs_[kind]
                            C.op("act", lambda e: e.copy(out=xt_[:, kind, 2:514], in_=ps[:]), reads=[bps], writes=[bx])
                            C.op("act", lambda e: e.activation(out=t_[:], in_=ps[:], func=AF.Identity, bias=fb, scale=fw(2)), reads=[bps, bsvt], writes=[bt])
                            ev.append((kind, fw, t_, bt, bx))
                        C.op("pool", lambda e: e.tensor_copy(out=halo[:, 2 * k:2 * k + 2, :], in_=xt_[:, :, 512:514]), reads=bxs_, writes=bh2)
                        for (kind, fw, t_, bt, bx) in ev:
                            C.op("dve", lambda e: e.scalar_tensor_tensor(out=t_[:], in0=xt_[:, kind, 1:513], scalar=fw(1), in1=t_[:], op0=ALU.mult, op1=ALU.add),
                                 reads=[bx, bxh, bsvt, bt], writes=[bt])
                            C.op("dve", lambda e: e.scalar_tensor_tensor(out=t_[:], in0=xt_[:, kind, 0:512], scalar=fw(0), in1=t_[:], op0=ALU.mult, op1=ALU.add),
                                 reads=[bx, bxh, bsvt, bt], writes=[bt])
                            res.append((t_, bt))
                        if pend is not None:
                            tail(*pend)
                        pend = (k, res[0][0], res[0][1], res[1][0], res[1][1])
                        if k in (6, 10, 14):
                            wd_load(g, (k - 6) // 4)
                        if k == 4 and deferred:
                            deferred.pop(0)()
                    tail(*pend)

                def wd_load(g, dc):
                    wd, bwd = wdnr.next()
                    C.dma("sp", wd[:], wdnv[:, dc, :, :], reads=wbufs(l, OFF_WDN, OFF_WDN + 8 * 2816), writes=[bwd])
                    G[g].setdefault("wd", {})[dc] = (wd, bwd)

                def C_mm(g, hook=None):
                    st = G[g]
                    h, bh = st["h"], st["bh"]
                    KS = NFC - 6

                    def getwd(dc):
                        if dc not in st.get("wd", {}):
                            wd_load(g, dc)
                        return st["wd"].pop(dc)

                    def part1(dc, ps, bps, wd, bwd):
                        C.mm(ps[:], [(wd[:, k, :], actt[:, k, :]) for k in range(KS)], reads=[bwd, brelay], writes=[bps], start=True, stop=False)

                    def part2(dc, ps, bps, wd, bwd):
                        C.mm(ps[:], [(wd[:, k, :], actt[:, k, :]) for k in range(KS, NFC)], reads=[bwd] + bact[KS:], writes=[bps], start=False, stop=True)
                        C.op("dve", lambda e: e.tensor_tensor(out=h[:, dc, :], in0=ps[:], in1=h[:, dc, :], op=ALU.add), reads=[bps, bh], writes=[bh])

                    first = []
                    for dc in (0, 1):
                        wd, bwd = getwd(dc)
                        ps, bps = pn.next()
                        part1(dc, ps, bps, wd, bwd)
                        first.append((dc, ps, bps, wd, bwd))
                    for i_, args in enumerate(first):
                        part2(*args)
                        wd_load(g, 3 + i_)
                    for dc in range(2, 8):
                        wd, bwd = getwd(dc)
                        ps, bps = pz.next()
                        part1(dc, ps, bps, wd, bwd)
                        part2(dc, ps, bps, wd, bwd)
                        if dc + 3 < 8:
                            wd_load(g, dc + 3)
                        if dc == 3 and hook is not None:
                            hook()

                def D_sq(g):
                    st = G[g]
                    st["sqs"] = sq_part([st["h"][:, c, :] for c in range(8)], [st["bh"]])

                def D_stat(g):
                    st = G[g]
                    h, bh = st["h"], st["bh"]
                    rs, brs = stat_part(st["sqs"], 1024)
                    hn, bhn = nbr.next()
                    scale_bf(hn, bhn, h, bh, rs, brs, range(8))
                    st["hn"], st["bhn"] = hn, bhn

                def D_mm(g, hook=None):
                    st = G[g]
                    h, bh, hn, bhn, pb, bpb = st["h"], st["bh"], st["hn"], st["bhn"], st["pb"], st["bpb"]
                    for dc in range(8):
                        ps, bps = pz.next()
                        C.mm(ps[:], [(wpg[:, kc, dc * 128:(dc + 1) * 128], hn[:, kc, :]) for kc in range(8)], reads=[bwpg, bhn], writes=[bps])
                        ps2, bps2 = pz.next()
                        C.mm(ps2[:], [(wple[:, kc, dc * 128:(dc + 1) * 128], pb[:, kc, :]) for kc in range(2)], reads=[bwple, bpb], writes=[bps2])
                        sg, bsg = sgr.next()
                        C.op("act", lambda e: e.activation(out=sg[:], in_=ps[:], func=AF.Sigmoid), reads=[bps], writes=[bsg])
                        C.op("dve", lambda e: e.tensor_tensor(out=sg[:], in0=ps2[:], in1=sg[:], op=ALU.mult), reads=[bps2, bsg], writes=[bsg])
                        C.op("pool", lambda e: e.tensor_tensor(out=h[:, dc, :], in0=sg[:], in1=h[:, dc, :], op=ALU.add), reads=[bsg, bh], writes=[bh])
                        if dc == 3 and hook is not None:
                            hook()
                    ts_fin = st["ts"]

                    def fin():
                        if last:
                            rs, brs = rms_bc([h[:, c, :] for c in range(8)], [bh], 1024, sq8, pn, lnr, rsr)
                            for c in range(8):
                                C.op("dve", lambda e: e.scalar_tensor_tensor(out=h[:, c, :], in0=h[:, c, :], scalar=svt[:, l, SV_FN + c:SV_FN + c + 1], in1=rs[:],
                                                                             op0=ALU.mult, op1=ALU.mult), reads=[bh, brs, bsvt], writes=[bh])
                        C.dma("pool", dview[:, :, ts_fin], h[:], reads=[bh], writes=[bdst])

                    deferred.append(fin)
                    del G[g]

                sq8 = Ring(C, es, "sq8p", [128, 512], BF16, 8)
                deferred = []
                A_sq(0)
                A_stat(0)
                A_mm(0, range(8))
                B_sq(0)
                B_stat(0)
                B_mm(0)
                for g in range(NG5):
                    nx = g + 1 < NG5
                    if nx:
                        A_sq(g + 1)
                    C_mm(g, hook=(lambda: A_stat(g + 1)) if nx else None)
                    D_sq(g)
                    if nx:
                        A_mm(g + 1, range(0, 4))
                    D_stat(g)
                    if nx:
                        A_mm(g + 1, range(4, 8))
                        B_sq(g + 1)
                    D_mm(g, hook=(lambda: B_stat(g + 1)) if nx else None)
                    if nx:
                        B_mm(g + 1)
                while deferred:
                    deferred.pop(0)()
                C.fence()
            es_w.close()
            hsrc, bhsrc = HT, bHT
        C.fence()
    return nc


_NC_CACHE = {}


def _host_inputs(inputs, S, L):
    w = {k: np.asarray(v) for k, v in inputs.items()}
    imgs, gs, svs = [], [], []
    for i in range(L):
        a, b, c = _weight_image(i, w)
        imgs.append(a)
        gs.append(b)
        svs.append(c)
    wimg = np.stack(imgs)
    gimg = np.stack(gs)
    svimg = np.stack(svs)
    cimg = _consts()
    B = w["x"].shape[0]
    maps = []
    for b in range(B):
        maps.append({
            "xT": np.ascontiguousarray(w["x"][b].T),
            "pT": np.ascontiguousarray(w["p"][:L, b].transpose(0, 2, 1)),
            "pos": np.ascontiguousarray(w["positions"][b].reshape(1, S).astype(np.int32)),
            "wimg": wimg, "gimg": gimg, "svimg": svimg, "cimg": cimg,
        })
    return maps


def kernel(**inputs):
    x = np.asarray(inputs["x"])
    B, S, _ = x.shape
    L = np.asarray(inputs["w_in"]).shape[0]
    key = (S, L)
    if key not in _NC_CACHE:
        _NC_CACHE[key] = build(S, L)
    nc = _NC_CACHE[key]
    maps = _host_inputs(inputs, S, L)
    res = run_bass_kernel_spmd(nc, maps, core_ids=list(range(B)))
    out = np.stack([np.ascontiguousarray(r["outT"].T) for r in res.results])
    return out.astype(np.float32)
```
